# Optimizing a Trainium2 kernel written in Bass

```python
import math
import jax, jax.numpy as jnp
from jax import lax
import numpy as np

D_MODEL = 1024
BATCH = 8
SEQ = 4096
DEPTH = 4

CHUNK = 64
EPS = 1e-6

S5_WIDTH = D_MODEL // 2
S5_GROUP = 16
S5_GROUPS = S5_WIDTH // S5_GROUP
S5_STATE = 64
DT_MIN = 1e-3
DT_MAX = 1e-1

GLA_HEADS = 4
GLA_KEY = D_MODEL // 2
GLA_VAL = D_MODEL
GLA_DK = GLA_KEY // GLA_HEADS
GLA_DV = GLA_VAL // GLA_HEADS
GLA_GATE_RANK = 16
GLA_GATE_TEMP = 16.0

D_FF = -(-8 * D_MODEL // (3 * 256)) * 256

IN_PROJ_SIZES = (S5_WIDTH, GLA_KEY, GLA_KEY, GLA_VAL, GLA_VAL, GLA_GATE_RANK, D_MODEL, D_MODEL)
D_IN_PROJ = S5_WIDTH + 2 * GLA_KEY + 2 * GLA_VAL + GLA_GATE_RANK + 2 * D_MODEL

kernel_name = "hybrid_s5_gla_gated_merge_trunk"


def rms_norm(x, g):
    x32 = x.astype(jnp.float32)
    y = x32 * lax.rsqrt(jnp.mean(x32 * x32, axis=-1, keepdims=True) + EPS)
    return (y * g.astype(jnp.float32)).astype(x.dtype)


def s5_branch(u, a_re, a_im, log_dt, b_re, b_im, c_re, c_im, d_skip, w_glu):
    f32 = jnp.float32
    bsz, seq, _ = u.shape
    u32 = u.astype(f32).reshape(bsz, seq, S5_GROUPS, S5_GROUP)
    a = lax.complex(a_re.astype(f32), a_im.astype(f32))
    dt = jnp.exp(log_dt.astype(f32))[:, None]
    a_bar = jnp.exp(a * dt)
    b = lax.complex(b_re.astype(f32), b_im.astype(f32))
    b_bar = ((a_bar - 1.0) / a)[..., None] * b
    bu = jnp.einsum('blgc,gpc->blgp', u32.astype(jnp.complex64), b_bar)
    decay = jnp.broadcast_to(a_bar, bu.shape)

    def combine(left, right):
        a_l, b_l = left
        a_r, b_r = right
        return a_r * a_l, a_r * b_l + b_r

    _, states = lax.associative_scan(combine, (decay, bu), axis=1)
    c = lax.complex(c_re.astype(f32), c_im.astype(f32))
    y = jnp.real(jnp.einsum('blgp,gcp->blgc', states, c)) \
        + d_skip.astype(f32).reshape(S5_GROUPS, S5_GROUP) * u32
    y = jax.nn.gelu(y.reshape(bsz, seq, S5_WIDTH))
    val, gate = jnp.split(y @ w_glu.astype(f32), 2, axis=-1)
    return (val * jax.nn.sigmoid(gate)).astype(u.dtype)


def gla_branch(q, k, v, g, a_low, w_gate_up, b_gate, head_norm_g):
    f32 = jnp.float32
    out_dtype = v.dtype
    bsz, seq, _ = q.shape
    n_chunks = seq // CHUNK

    def chunked(t, d):
        return t.astype(f32).reshape(bsz, n_chunks, CHUNK, GLA_HEADS, d)

    qc = chunked(q, GLA_DK) * (GLA_DK ** -0.5)
    kc = chunked(k, GLA_DK)
    vc = chunked(v, GLA_DV)
    log_alpha = jax.nn.log_sigmoid(a_low.astype(f32) @ w_gate_up.astype(f32)
                                   + b_gate.astype(f32)) / GLA_GATE_TEMP
    log_alpha = chunked(log_alpha, GLA_DK)
    cum = jnp.cumsum(log_alpha, axis=2)
    total = cum[:, :, -1:]
    k_end = kc * jnp.exp(total - cum)

    scores = jnp.einsum('bcqhk,bcshk->bchqs', qc, k_end)
    intra = jnp.einsum('bchqs,bcshv->bcqhv', scores, vc)

    kv = jnp.einsum('bcshk,bcshv->bchkv', k_end, vc)
    chunk_decay = jnp.exp(total[:, :, 0])

    def step(state, xs):
        dec, kv_c = xs
        return dec[..., None] * state + kv_c, state

    init = jnp.zeros((bsz, GLA_HEADS, GLA_DK, GLA_DV), f32)
    _, prev = lax.scan(step, init, (jnp.moveaxis(chunk_decay, 1, 0), jnp.moveaxis(kv, 1, 0)))
    prev = jnp.moveaxis(prev, 0, 1)
    inter = jnp.einsum('bcqhk,bchkv->bcqhv', qc * jnp.exp(total), prev)

    o = intra + inter
    o = o * lax.rsqrt(jnp.mean(o * o, axis=-1, keepdims=True) + EPS)
    o = o * head_norm_g.astype(f32).reshape(GLA_HEADS, GLA_DV)
    o = o.reshape(bsz, seq, GLA_VAL) * jax.nn.silu(g.astype(f32))
    return o.astype(out_dtype)


def setup_inputs(seed: int = 0) -> dict:
    key = jax.random.key(seed)
    ks = jax.random.split(key, 24)
    f32 = jnp.float32
    nrm = lambda k, shape, scale: scale * jax.random.normal(k, shape, f32)
    gain = lambda k, shape: 1.0 + 0.02 * jax.random.normal(k, shape, f32)
    a_im_init = jnp.pi * jnp.arange(S5_STATE, dtype=f32)
    return {
        "x": jax.random.normal(ks[0], (BATCH, SEQ, D_MODEL), f32),
        "attn_norm_g": gain(ks[1], (DEPTH, D_MODEL)),
        "w_in": nrm(ks[2], (DEPTH, D_MODEL, D_IN_PROJ), D_MODEL ** -0.5),
        "s5_a_re": -0.5 + 0.01 * jax.random.normal(ks[3], (DEPTH, S5_GROUPS, S5_STATE), f32),
        "s5_a_im": a_im_init + 0.01 * jax.random.normal(ks[4], (DEPTH, S5_GROUPS, S5_STATE), f32),
        "s5_log_dt": jax.random.uniform(ks[5], (DEPTH, S5_GROUPS), f32,
                                        minval=math.log(DT_MIN), maxval=math.log(DT_MAX)),
        "s5_b_re": nrm(ks[6], (DEPTH, S5_GROUPS, S5_STATE, S5_GROUP), (2 * S5_GROUP) ** -0.5),
        "s5_b_im": nrm(ks[7], (DEPTH, S5_GROUPS, S5_STATE, S5_GROUP), (2 * S5_GROUP) ** -0.5),
        "s5_c_re": nrm(ks[8], (DEPTH, S5_GROUPS, S5_GROUP, S5_STATE), S5_STATE ** -0.5),
        "s5_c_im": nrm(ks[9], (DEPTH, S5_GROUPS, S5_GROUP, S5_STATE), S5_STATE ** -0.5),
        "s5_d": nrm(ks[10], (DEPTH, S5_WIDTH), 1.0),
        "s5_w_glu": nrm(ks[11], (DEPTH, S5_WIDTH, 2 * S5_WIDTH), S5_WIDTH ** -0.5),
        "gla_w_gate_up": nrm(ks[12], (DEPTH, GLA_GATE_RANK, GLA_KEY), GLA_GATE_RANK ** -0.5),
        "gla_b_gate": nrm(ks[13], (DEPTH, GLA_KEY), 0.1),
        "gla_head_norm_g": gain(ks[14], (DEPTH, GLA_VAL)),
        "w_branch_s5": nrm(ks[15], (DEPTH, S5_WIDTH, D_MODEL), S5_WIDTH ** -0.5),
        "w_branch_gla": nrm(ks[16], (DEPTH, GLA_VAL, D_MODEL), GLA_VAL ** -0.5),
        "w_out": nrm(ks[17], (DEPTH, D_MODEL, D_MODEL), D_MODEL ** -0.5),
        "ffn_norm_g": gain(ks[18], (DEPTH, D_MODEL)),
        "w_ffn_gate": nrm(ks[19], (DEPTH, D_MODEL, D_FF), D_MODEL ** -0.5),
        "w_ffn_up": nrm(ks[20], (DEPTH, D_MODEL, D_FF), D_MODEL ** -0.5),
        "w_ffn_down": nrm(ks[21], (DEPTH, D_FF, D_MODEL), D_FF ** -0.5),
        "final_norm_g": gain(ks[22], (D_MODEL,)),
    }


def reference(x, attn_norm_g, w_in, s5_a_re, s5_a_im, s5_log_dt, s5_b_re, s5_b_im, s5_c_re, s5_c_im,
              s5_d, s5_w_glu, gla_w_gate_up, gla_b_gate, gla_head_norm_g, w_branch_s5, w_branch_gla,
              w_out, ffn_norm_g, w_ffn_gate, w_ffn_up, w_ffn_down, final_norm_g):
    split_points = [int(p) for p in np.cumsum(IN_PROJ_SIZES)[:-1]]
    h = x
    for layer in range(DEPTH):
        xn = rms_norm(h, attn_norm_g[layer])
        proj = xn @ w_in[layer]
        u, q, k, v, g, a_low, gate_s5, gate_gla = jnp.split(proj, split_points, axis=-1)
        y_s5 = s5_branch(u, s5_a_re[layer], s5_a_im[layer], s5_log_dt[layer], s5_b_re[layer],
                         s5_b_im[layer], s5_c_re[layer], s5_c_im[layer], s5_d[layer], s5_w_glu[layer])
        y_gla = gla_branch(q, k, v, g, a_low, gla_w_gate_up[layer], gla_b_gate[layer],
                           gla_head_norm_g[layer])
        mixed = jax.nn.sigmoid(gate_s5) * (y_s5 @ w_branch_s5[layer]) \
            + jax.nn.sigmoid(gate_gla) * (y_gla @ w_branch_gla[layer])
        h = h + mixed @ w_out[layer]
        hn = rms_norm(h, ffn_norm_g[layer])
        h = h + (jax.nn.silu(hn @ w_ffn_gate[layer]) * (hn @ w_ffn_up[layer])) @ w_ffn_down[layer]
    return rms_norm(h, final_norm_g)
```

```python
import contextlib
import numpy as np
import concourse.bass as bass
import concourse.mybir as mybir
from concourse.bass_utils import run_bass_kernel_spmd

F32 = mybir.dt.float32
BF16 = mybir.dt.bfloat16
I32 = mybir.dt.int32
U8 = mybir.dt.uint8
AF = mybir.ActivationFunctionType
ALU = mybir.AluOpType

D = 1024
L = 4096
DEPTH = 4
KT = 8
TB = 512
NBLK = L // TB
DFF = 2816
FT = DFF // 128
DIN = 5648
EPS = 1e-6
O_U, O_Q, O_K, O_V, O_G, O_A, O_GS5, O_GG = 0, 512, 1024, 1536, 2560, 3584, 3600, 4624
NM = 40
TWO_PI = 6.28318
SAME_SYNC = True

WEIGHT_NAMES = ["w_in", "s5_w_glu", "w_branch_s5", "w_branch_gla", "w_out", "w_ffn_gate", "w_ffn_up", "w_ffn_down"]
WSHAPES = {"w_in": (D, DIN), "s5_w_glu": (512, 1024), "w_branch_s5": (512, 1024), "w_branch_gla": (1024, 1024),
           "w_out": (1024, 1024), "w_ffn_gate": (D, DFF), "w_ffn_up": (D, DFF), "w_ffn_down": (DFF, D)}


class Buf:
    __slots__ = ("name", "w", "r", "excl", "guard")

    def __init__(self, name="", excl=False, guard=None):
        self.name = name
        self.w = {}
        self.r = {}
        self.excl = excl
        self.guard = guard


def _with_guards(reads, writes):
    extra = []
    for b in list(reads) + list(writes):
        g = b.guard
        if g is not None and g not in writes and g not in extra:
            extra.append(g)
    return list(reads) + extra


class Eng:
    def __init__(self, name, obj, sem, same_sync):
        self.name, self.obj, self.sem, self.same_sync = name, obj, sem, same_sync
        self.count = 0
        self.known = {}


class DmaQ:
    def __init__(self, eng, sems):
        self.eng, self.sems = eng, sems
        self.n = 0


class KB:
    def __init__(self, nc, es):
        self.nc = nc
        mk = lambda n: es.enter_context(nc.semaphore(n))
        self.eng = {
            "pe": Eng("pe", nc.tensor, mk("s_pe"), False),
            "act": Eng("act", nc.scalar, mk("s_act"), SAME_SYNC),
            "dve": Eng("dve", nc.vector, mk("s_dve"), SAME_SYNC),
            "pool": Eng("pool", nc.gpsimd, mk("s_pool"), SAME_SYNC),
            "sp": Eng("sp", nc.sync, mk("s_sp"), False),
        }
        self.q = {
            "sp": DmaQ(self.eng["sp"], [mk("d_sp%d" % i) for i in range(12)]),
            "pool": DmaQ(self.eng["pool"], [mk("d_pl%d" % i) for i in range(8)]),
        }
        self.n_inst = 0

    def _waits(self, E, reads, writes, partial):
        need = {}

        def add(d):
            for s, v in d.items():
                if need.get(s, (None, 0))[1] < v:
                    need[s] = (s, v)

        for b in reads:
            add(b.w)
            if b.excl:
                add({s_: v_ for s_, v_ in b.r.items() if s_ is not E.sem})
        for b in writes:
            add(b.r)
            if not partial:
                add(b.w)
        for s, v in need.values():
            if s is E.sem and not E.same_sync:
                continue
            if E.known.get(id(s), 0) >= v:
                continue
            E.obj.wait_ge(s, v)
            E.known[id(s)] = v
            self.n_inst += 1

    def _mark(self, tok_sem, tok_val, reads, writes, partial):
        for b in reads:
            if b.r.get(tok_sem, 0) < tok_val:
                b.r[tok_sem] = tok_val
        for b in writes:
            if partial:
                b.w[tok_sem] = tok_val
            else:
                b.w = {tok_sem: tok_val}
                b.r = {}

    def op(self, eng, fn, reads=(), writes=(), partial=False):
        E = self.eng[eng]
        reads = _with_guards(reads, writes)
        self._waits(E, reads, writes, partial)
        inst = fn(E.obj)
        E.count += 1
        inst.then_inc(E.sem, 1)
        self.n_inst += 1
        self._mark(E.sem, E.count, reads, writes, partial)
        return inst

    def mm(self, mms, reads=(), writes=(), partial=False):
        E = self.eng["pe"]
        reads = _with_guards(reads, writes)
        self._waits(E, reads, writes, partial)
        inst = None
        for f in mms:
            inst = f(E.obj)
            self.n_inst += 1
        E.count += 1
        inst.then_inc(E.sem, 1)
        self._mark(E.sem, E.count, reads, writes, partial)

    def dma(self, qn, out, in_, reads=(), writes=(), partial=False, **kw):
        Q = self.q[qn]
        E = Q.eng
        reads = _with_guards(reads, writes)
        self._waits(E, reads, writes, partial)
        slot = Q.n % len(Q.sems)
        target = 16 * (Q.n // len(Q.sems) + 1)
        sem = Q.sems[slot]
        if target > 16 and E.known.get(id(sem), 0) < target - 16:
            E.obj.wait_ge(sem, target - 16)
            E.known[id(sem)] = target - 16
        Q.n += 1
        E.obj.dma_start(out=out, in_=in_, **kw).then_inc(sem, 16)
        self.n_inst += 1
        self._mark(sem, target, reads, writes, partial)

    def barrier(self, full=False):
        toks = [(E.sem, E.count) for E in self.eng.values() if E.count > 0]
        for Q in (self.q.values() if full else ()):
            nq = len(Q.sems)
            for i, s in enumerate(Q.sems):
                cnt = (Q.n - i + nq - 1) // nq if Q.n > i else 0
                if cnt > 0:
                    toks.append((s, 16 * cnt))
        for E in self.eng.values():
            if E.name == "sp" and not full:
                continue
            for s, v in toks:
                if s is E.sem:
                    continue
                if E.known.get(id(s), 0) >= v:
                    continue
                E.obj.wait_ge(s, v)
                E.known[id(s)] = v
                self.n_inst += 1


class Arena:
    def __init__(self, tensor, nbytes):
        self.t, self.n, self.off = tensor, nbytes, 0

    def reset(self):
        self.off = 0

    def alloc(self, shape_free, dtype, nparts=128):
        esz = {F32: 4, BF16: 2, I32: 4}[dtype]
        n = esz
        for s in shape_free:
            n *= s
        off = (self.off + 31) // 32 * 32
        assert off + n <= self.n, ("arena overflow", off, n, self.n)
        self.off = off + n
        ap = self.t[0:nparts, off:off + n].bitcast(dtype)
        if len(shape_free) == 2:
            ap = ap.rearrange("p (a b) -> p a b", a=shape_free[0])
        elif len(shape_free) == 3:
            ap = ap.rearrange("p (a b c) -> p a b c", a=shape_free[0], b=shape_free[1])
        elif len(shape_free) == 4:
            ap = ap.rearrange("p (a b c d) -> p a b c d", a=shape_free[0], b=shape_free[1], c=shape_free[2])
        return ap


def make_consts():
    c = {}
    c["ident"] = np.eye(128, dtype=np.float32)
    s = np.arange(128)[:, None]
    t = np.arange(128)[None, :]
    c["mrev"] = ((s > t) & (s // 64 == t // 64)).astype(np.float32)
    c["cind"] = np.stack([(np.arange(128) < 64), (np.arange(128) >= 64)], 1).astype(np.float32)
    c["mvals"] = np.tile(np.arange(-3, NM - 3, dtype=np.float32)[None, :], (128, 1))
    s4 = (np.arange(128) // 32)[:, None, None]
    m = np.arange(32)[None, :, None]
    c["tabmask"] = np.broadcast_to((m >= s4), (128, 32, 32)).astype(np.float32).reshape(128, 1024)
    order = ["ident", "mrev", "cind", "mvals", "tabmask"]
    offs, o = {}, 0
    for k in order:
        offs[k] = (o, c[k].shape[1])
        o += c[k].shape[1]
    return np.concatenate([c[k] for k in order], axis=1), offs


CONSTS_NP, CONST_OFFS = make_consts()


class _Stop(Exception):
    pass


def build_program(n_layers=DEPTH, final_norm=True, dumps=(), stop=None):
    nc = bass.Bass("TRN2", target_bir_lowering=False)
    es = contextlib.ExitStack()
    dt_in = lambda name, shape: nc.dram_tensor(name, list(shape), F32, kind="ExternalInput").ap()
    xT = dt_in("xT", (D, L))
    consts_d = dt_in("consts", CONSTS_NP.shape)
    P = {}
    for name, shape in [("attn_norm_g", (DEPTH, D)), ("w_in", (DEPTH, D, DIN)), ("s5_a_re", (DEPTH, 32, 64)),
                        ("s5_a_im", (DEPTH, 32, 64)), ("s5_log_dt", (DEPTH, 32)), ("s5_b_re", (DEPTH, 32, 64, 16)),
                        ("s5_b_im", (DEPTH, 32, 64, 16)), ("s5_c_re", (DEPTH, 32, 16, 64)),
                        ("s5_c_im", (DEPTH, 32, 16, 64)), ("s5_d", (DEPTH, 512)), ("s5_w_glu", (DEPTH, 512, 1024)),
                        ("gla_w_gate_up", (DEPTH, 16, 512)), ("gla_b_gate", (DEPTH, 512)),
                        ("gla_head_norm_g", (DEPTH, 1024)), ("w_branch_s5", (DEPTH, 512, 1024)),
                        ("w_branch_gla", (DEPTH, 1024, 1024)), ("w_out", (DEPTH, 1024, 1024)),
                        ("ffn_norm_g", (DEPTH, D)), ("w_ffn_gate", (DEPTH, D, DFF)), ("w_ffn_up", (DEPTH, D, DFF)),
                        ("w_ffn_down", (DEPTH, DFF, D)), ("final_norm_g", (D,))]:
        P[name] = dt_in(name, shape)
    outT = nc.dram_tensor("outT", [D, L], F32, kind="ExternalOutput").ap()
    hT = nc.dram_tensor("hT_scr", [D, L], F32, kind="Internal").ap()
    WB = {}
    for name in WEIGHT_NAMES:
        r, c = WSHAPES[name]
        WB[name] = nc.dram_tensor("wb_" + name, [n_layers, r, c], BF16, kind="Internal").ap()
    dump_aps = {}
    for name, shape in dumps:
        dump_aps[name] = nc.dram_tensor("dbg_" + name, list(shape), F32, kind="ExternalOutput").ap()

    es.enter_context(nc.Block())
    kb = KB(nc, es)
    op, mm, dma = kb.op, kb.mm, kb.dma
    es.enter_context(nc.allow_non_contiguous_dma(reason="small param loads"))

    sb = lambda name, shape, dt: nc.alloc_sbuf_tensor(name, list(shape), dt)
    cst = sb("cst", CONSTS_NP.shape, F32)
    cst_b = Buf("cst")

    def cview(k):
        o, n = CONST_OFFS[k]
        return cst[:, o:o + n]

    ident_f = cview("ident")
    ident_b = sb("ident_b", (128, 128), BF16)
    ones_b = sb("ones_b", (128, 128), BF16)
    mrev = cview("mrev")
    cind = cview("cind")
    mvals = cview("mvals")
    tabmask = cview("tabmask")
    g1col = sb("g1col", (128, DEPTH, KT), F32)
    g2col = sb("g2col", (128, DEPTH, KT), F32)
    hngcol = sb("hngcol", (128, DEPTH, KT), F32)
    gfcol = sb("gfcol", (128, KT), F32)
    small_b = Buf("small")
    yfm = sb("yfm", (128, 4, L), BF16)
    yfm_b = [Buf("yfm%d" % i) for i in range(4)]
    Sst = sb("Sst", (128, 4, 256), F32)
    Sst_b = [Buf("S%d" % i) for i in range(4)]
    alow = sb("alow", (32, TB), BF16)
    alow_b = Buf("alow")
    alowz_b = Buf("alowz")
    PS = [nc.alloc_psum_tensor("ps%d" % i, [128, 512], F32) for i in range(8)]
    PSb = [Buf("ps%d" % i, excl=True) for i in range(8)]

    ARENA_BYTES = nc.sbuf_bytes_remaining - 256
    arena_t = sb("arena", (128, ARENA_BYTES), U8)
    ar = Arena(arena_t, ARENA_BYTES)

    def dump(name, ap, rbufs):
        if name in dump_aps:
            dma("pool", dump_aps[name], ap, reads=rbufs)

    dma("sp", cst[:, :], consts_d[:, :], writes=[cst_b])
    io_b = Buf("ident_ones")
    op("dve", lambda e: e.tensor_copy(out=ident_b[:, :], in_=ident_f), reads=[cst_b], writes=[io_b])
    op("dve", lambda e: e.memset(ones_b[:, :], 1.0), writes=[io_b], partial=True)
    op("dve", lambda e: e.memset(alow[:, :], 1.0), writes=[alow_b, alowz_b])
    for l in range(DEPTH):
        dma("sp", g1col[:, l, :], P["attn_norm_g"][l].rearrange("(k p) -> p k", p=128), writes=[small_b], partial=True)
        dma("sp", g2col[:, l, :], P["ffn_norm_g"][l].rearrange("(k p) -> p k", p=128), writes=[small_b], partial=True)
        dma("sp", hngcol[:, l, :], P["gla_head_norm_g"][l].rearrange("(k p) -> p k", p=128), writes=[small_b], partial=True)
    dma("sp", gfcol[:, :], P["final_norm_g"].rearrange("(k p) -> p k", p=128), writes=[small_b], partial=True)
    wcast_bs = [Buf("wcast%d" % l) for l in range(n_layers)]
    wu_bs = [Buf("wu%d" % l) for l in range(n_layers)]
    WBu = nc.dram_tensor("wb_u", [n_layers, D, 512], BF16, kind="Internal").ap()

    def cast_wu(l):
        dma("pool", WBu[l], P["w_in"][l][:, O_U:O_U + 512], writes=[wu_bs[l]])

    def cast_pieces(l, step):
        out = []
        for name in WEIGHT_NAMES:
            r, c = WSHAPES[name]
            rows = r * c // 1024
            src = P[name][l].rearrange("r c -> (r c)").rearrange("(a b) -> a b", b=1024)
            dst = WB[name][l].rearrange("r c -> (r c)").rearrange("(a b) -> a b", b=1024)
            for r0 in range(0, rows, step):
                r1 = min(rows, r0 + step)
                out.append(lambda src=src, dst=dst, r0=r0, r1=r1: dma("pool", dst[r0:r1, :], src[r0:r1, :], writes=[wcast_bs[l]], partial=True))
        return out

    hd_b = [Buf("hT%d" % i) for i in range(NBLK)]
    cast_wu(0)
    for f in cast_pieces(0, 8192):
        f()
    cast_q = []

    NSLOT = 4
    SLOT_BYTES = 8192
    ring = {"t": None, "b": None, "n": 0, "wc": None}

    def wload(src, kt, ncols):
        i = ring["n"] % NSLOT
        ring["n"] += 1
        assert kt * ncols * 2 <= SLOT_BYTES
        ap = ring["t"][i][:, 0, 0:kt * ncols].rearrange("p (k c) -> p k c", k=kt)
        dma("sp", ap, src.rearrange("(k p) c -> p k c", p=128), reads=[ring["wc"]], writes=[ring["b"][i]])
        return ap, ring["b"][i]

    def rms_block(hsrc, hbuf, gcol_ap, sqt, sq_b, psi, rs_out, rs_b, xn, xn_b, tmp, tmp_b, perm=False):
        for k in range(KT):
            op("act", lambda e, k=k: e.activation(out=sqt[:, k, :], in_=hsrc[:, k, :], func=AF.Square),
               reads=[hbuf], writes=[sq_b], partial=(k > 0))
        mm([lambda e, k=k: e.matmul(PS[psi][:, :], lhsT=ones_b[:, :], rhs=sqt[:, k, :], start=(k == 0), stop=(k == KT - 1))
            for k in range(KT)], reads=[sq_b, small_b, io_b], writes=[PSb[psi]])
        op("act", lambda e: e.activation(out=tmp, in_=PS[psi][:, :], func=AF.Sqrt, scale=1.0 / D, bias=EPS),
           reads=[PSb[psi]], writes=[tmp_b])
        op("dve", lambda e: e.reciprocal(out=rs_out, in_=tmp), reads=[tmp_b], writes=[rs_b])
        if xn is not None:
            make_xn(hsrc, hbuf, gcol_ap, rs_out, rs_b, xn, xn_b, perm)

    def make_xn(hsrc, hbuf, gcol_ap, rs, rs_b, xn, xn_b, perm=False):
        nat = lambda a: a.rearrange("p (c q) -> p q c", q=32)
        prm = lambda a: a.rearrange("p (q c) -> p q c", q=32)
        for k in range(KT):
            if perm:
                o_, i0, i1 = prm(xn[:, k, :]), nat(hsrc[:, k, :]), nat(rs)
            else:
                o_, i0, i1 = xn[:, k, :], hsrc[:, k, :], rs
            op("dve", lambda e, k=k, o_=o_, i0=i0, i1=i1: e.scalar_tensor_tensor(out=o_, in0=i0, scalar=gcol_ap[:, k:k + 1],
                                                                               in1=i1, op0=ALU.mult, op1=ALU.mult),
               reads=[hbuf, rs_b, small_b], writes=[xn_b[k]])

    def chk(name):
        if stop == name:
            raise _Stop()

    try:
      chk("setup")
      for l in range(n_layers):
        hsrc_d = xT if l == 0 else hT
        last = (l == n_layers - 1)
        kb.barrier(full=True)
        ar.reset()
        CTre = ar.alloc((16, 16), F32)
        CTim = ar.alloc((16, 16), F32)
        Dcol = ar.alloc((1, 16), F32)
        PWre = ar.alloc((16, NM), F32)
        PWim = ar.alloc((16, NM), F32)
        nPWim = ar.alloc((16, NM), F32)
        Epre = ar.alloc((16, 4, 2, 16), BF16)
        Epim = ar.alloc((16, 4, 2, 16), BF16)
        CAB = ar.alloc((2, 2, 16), F32)
        p_mark = ar.off
        Ain = ar.alloc((3, 128), F32, nparts=16)
        ldt16 = ar.alloc((1, 2), F32, nparts=16)
        AT = ar.alloc((3, 16), F32)
        Bre = ar.alloc((16, 16), F32)
        Bim = ar.alloc((16, 16), F32)
        Cin = ar.alloc((2, 2, 128), F32)
        t40a = ar.alloc((16, NM), F32)
        t40b = ar.alloc((16, NM), F32)
        t40c = ar.alloc((16, NM), F32)
        t40i = ar.alloc((16, NM), I32)
        s16 = [ar.alloc((1, 16), F32) for _ in range(10)]
        gre = ar.alloc((16, 4), F32)
        gim = ar.alloc((16, 4), F32)
        g4a = ar.alloc((16, 4), F32)
        g4b = ar.alloc((16, 4), F32)
        e1 = ar.alloc((16, 16), F32)
        e2 = ar.alloc((16, 16), F32)
        prm_b = Buf("prm")
        pw_b = Buf("pw")
        ep_b = Buf("ep")

        dma("sp", Ain[:, 0, :], P["s5_a_re"][l].rearrange("(a g) p -> a (g p)", g=2), writes=[prm_b], partial=True)
        dma("sp", Ain[:, 1, :], P["s5_a_im"][l].rearrange("(a g) p -> a (g p)", g=2), writes=[prm_b], partial=True)
        dma("sp", ldt16[:, 0, :], P["s5_log_dt"][l].rearrange("(a g) -> a g", g=2), writes=[prm_b], partial=True)
        for g2 in range(2):
            dma("sp", Bre[64 * g2:64 * g2 + 64, :, :], P["s5_b_re"][l].rearrange("(a g) p c -> g p a c", g=2)[g2],
                writes=[prm_b], partial=True)
            dma("sp", Bim[64 * g2:64 * g2 + 64, :, :], P["s5_b_im"][l].rearrange("(a g) p c -> g p a c", g=2)[g2],
                writes=[prm_b], partial=True)
        for ri, nm in enumerate(("s5_c_re", "s5_c_im")):
            for half in range(2):
                for a8 in range(8):
                    pi = half * 8 + a8
                    dma("sp", Cin[16 * a8:16 * a8 + 16, ri, half, :].rearrange("c (g p) -> c g p", g=2),
                        P[nm][l][2 * pi:2 * pi + 2].rearrange("g c p -> c g p"), writes=[prm_b], partial=True)
        for s4 in range(4):
            dma("sp", Dcol[32 * s4:32 * s4 + 32, 0, :], P["s5_d"][l].rearrange("(a q) -> q a", q=32),
                writes=[prm_b], partial=True)
        op("dve", lambda e: e.tensor_copy(out=Ain[:, 2, :].rearrange("a (g p) -> a g p", g=2),
                                          in_=ldt16[:, 0, :].unsqueeze(2).to_broadcast([16, 2, 64])),
           reads=[prm_b], writes=[prm_b])
        mm([lambda e, i=i: e.transpose(PS[0][:, 16 * i:16 * i + 16], Ain[:, i, :], ident_f[0:16, 0:16]) for i in range(3)],
           reads=[prm_b, cst_b], writes=[PSb[0]])
        op("dve", lambda e: e.tensor_copy(out=AT[:, :, :], in_=PS[0][:, 0:48].rearrange("p (a b) -> p a b", a=3)),
           reads=[PSb[0]], writes=[prm_b], partial=True)
        for ri, CT in enumerate((CTre, CTim)):
            mm([lambda e, h=h, ri=ri: e.transpose(PS[1][:, 128 * h:128 * h + 128], Cin[:, ri, h, :], ident_f)
                for h in range(2)], reads=[prm_b, cst_b], writes=[PSb[1]])
            op("dve", lambda e, CT=CT: e.tensor_copy(out=CT[:, :, :], in_=PS[1][:, 0:256].rearrange("p (a c) -> p a c", c=16)),
               reads=[PSb[1]], writes=[prm_b], partial=True)
        are, aim, ldt = AT[:, 0, :], AT[:, 1, :], AT[:, 2, :]
        dtv, lam, om, den, rden, abr, cre, cim, tA, tB = [s[:, 0, :] for s in s16]
        pb = [prm_b]
        op("act", lambda e: e.activation(out=dtv, in_=ldt, func=AF.Exp), reads=pb, writes=pb)
        op("dve", lambda e: e.tensor_tensor(out=lam, in0=are, in1=dtv, op=ALU.mult), reads=pb, writes=pb)
        op("dve", lambda e: e.tensor_tensor(out=om, in0=aim, in1=dtv, op=ALU.mult), reads=pb, writes=pb)
        bc_m = lambda a: a.unsqueeze(2).to_broadcast([128, 16, NM])
        mv_bc = mvals.unsqueeze(1).to_broadcast([128, 16, NM])
        op("dve", lambda e: e.tensor_tensor(out=t40a[:, :, :], in0=bc_m(lam), in1=mv_bc, op=ALU.mult), reads=pb + [cst_b], writes=[pw_b])
        op("act", lambda e: e.activation(out=t40a[:, :, :], in_=t40a[:, :, :], func=AF.Exp), reads=[pw_b], writes=[pw_b])
        op("dve", lambda e: e.tensor_tensor(out=t40b[:, :, :], in0=bc_m(om), in1=mv_bc, op=ALU.mult), reads=pb + [cst_b], writes=[pw_b])
        op("dve", lambda e: e.tensor_scalar(out=t40b[:, :, :], in0=t40b[:, :, :], scalar1=1.0 / (2 * np.pi), scalar2=None, op0=ALU.mult),
           reads=[pw_b], writes=[pw_b])
        for (shift, dst, neg) in ((0.0, PWim, nPWim), (0.25, PWre, None)):
            src = t40b
            if shift != 0.0:
                op("dve", lambda e: e.tensor_scalar(out=t40c[:, :, :], in0=t40b[:, :, :], scalar1=shift, scalar2=None, op0=ALU.add),
                   reads=[pw_b], writes=[pw_b])
                src = t40c
            op("dve", lambda e, src=src: e.tensor_copy(out=t40i[:, :, :], in_=src[:, :, :]), reads=[pw_b], writes=[pw_b])
            op("dve", lambda e, dst=dst: e.tensor_copy(out=dst[:, :, :], in_=t40i[:, :, :]), reads=[pw_b], writes=[pw_b])
            op("dve", lambda e, src=src, dst=dst: e.tensor_tensor(out=dst[:, :, :], in0=src[:, :, :], in1=dst[:, :, :], op=ALU.subtract),
               reads=[pw_b], writes=[pw_b])
            op("act", lambda e, dst=dst: e.activation(out=dst[:, :, :], in_=dst[:, :, :], func=AF.Sin, scale=TWO_PI), reads=[pw_b], writes=[pw_b])
            op("dve", lambda e, dst=dst: e.tensor_tensor(out=dst[:, :, :], in0=dst[:, :, :], in1=t40a[:, :, :], op=ALU.mult),
               reads=[pw_b], writes=[pw_b])
        op("dve", lambda e: e.tensor_scalar(out=nPWim[:, :, :], in0=PWim[:, :, :], scalar1=-1.0, scalar2=None, op0=ALU.mult),
           reads=[pw_b], writes=[pw_b])
        MI = lambda m: m + 3
        rw = dict(reads=pb + [pw_b], writes=pb)
        op("dve", lambda e: e.tensor_scalar(out=abr, in0=PWre[:, :, MI(1)], scalar1=-1.0, scalar2=None, op0=ALU.add), **rw)
        abi = PWim[:, :, MI(1)]
        op("dve", lambda e: e.tensor_tensor(out=den, in0=are, in1=are, op=ALU.mult), **rw)
        op("dve", lambda e: e.tensor_tensor(out=tA, in0=aim, in1=aim, op=ALU.mult), **rw)
        op("dve", lambda e: e.tensor_tensor(out=den, in0=den, in1=tA, op=ALU.add), **rw)
        op("dve", lambda e: e.reciprocal(out=rden, in_=den), **rw)
        op("dve", lambda e: e.tensor_tensor(out=tA, in0=abr, in1=are, op=ALU.mult), **rw)
        op("dve", lambda e: e.tensor_tensor(out=tB, in0=abi, in1=aim, op=ALU.mult), **rw)
        op("dve", lambda e: e.tensor_tensor(out=tA, in0=tA, in1=tB, op=ALU.add), **rw)
        op("dve", lambda e: e.tensor_tensor(out=cre, in0=tA, in1=rden, op=ALU.mult), **rw)
        op("dve", lambda e: e.tensor_tensor(out=tA, in0=abi, in1=are, op=ALU.mult), **rw)
        op("dve", lambda e: e.tensor_tensor(out=tB, in0=abr, in1=aim, op=ALU.mult), **rw)
        op("dve", lambda e: e.tensor_tensor(out=tA, in0=tA, in1=tB, op=ALU.subtract), **rw)
        op("dve", lambda e: e.tensor_tensor(out=cim, in0=tA, in1=rden, op=ALU.mult), **rw)
        bc4 = lambda a: a.unsqueeze(2).to_broadcast([128, 16, 4])
        pre4, pim4 = PWre[:, :, MI(0):MI(4)], PWim[:, :, MI(0):MI(4)]
        op("dve", lambda e: e.tensor_tensor(out=g4a[:, :, :], in0=pre4, in1=bc4(cre), op=ALU.mult), **rw)
        op("dve", lambda e: e.tensor_tensor(out=g4b[:, :, :], in0=pim4, in1=bc4(cim), op=ALU.mult), **rw)
        op("dve", lambda e: e.tensor_tensor(out=gre[:, :, :], in0=g4a[:, :, :], in1=g4b[:, :, :], op=ALU.subtract), **rw)
        op("dve", lambda e: e.tensor_tensor(out=g4a[:, :, :], in0=pre4, in1=bc4(cim), op=ALU.mult), **rw)
        op("dve", lambda e: e.tensor_tensor(out=g4b[:, :, :], in0=pim4, in1=bc4(cre), op=ALU.mult), **rw)
        op("dve", lambda e: e.tensor_tensor(out=gim[:, :, :], in0=g4a[:, :, :], in1=g4b[:, :, :], op=ALU.add), **rw)
        op("pool", lambda e: e.memset(Epre[:, :, :, :, :], 0.0), writes=[ep_b])
        op("pool", lambda e: e.memset(Epim[:, :, :, :, :], 0.0), writes=[ep_b], partial=True)
        for s4 in range(4):
            m_ = 3 - s4
            bc16 = lambda a: a.unsqueeze(2).to_broadcast([128, 16, 16])
            gr, gi = gre[:, :, m_], gim[:, :, m_]
            rwe = dict(reads=pb + [ep_b], writes=[ep_b])
            op("dve", lambda e: e.tensor_tensor(out=e1[:, :, :], in0=Bre[:, :, :], in1=bc16(gr), op=ALU.mult), **rwe)
            op("dve", lambda e: e.tensor_tensor(out=e2[:, :, :], in0=Bim[:, :, :], in1=bc16(gi), op=ALU.mult), **rwe)
            for g2 in range(2):
                ps_ = slice(64 * g2, 64 * g2 + 64)
                op("dve", lambda e, ps_=ps_, g2=g2, s4=s4: e.tensor_tensor(out=Epre[ps_, :, s4, g2, :], in0=e1[ps_, :, :], in1=e2[ps_, :, :],
                                                                           op=ALU.subtract), **rwe)
            op("dve", lambda e: e.tensor_tensor(out=e1[:, :, :], in0=Bim[:, :, :], in1=bc16(gr), op=ALU.mult), **rwe)
            op("dve", lambda e: e.tensor_tensor(out=e2[:, :, :], in0=Bre[:, :, :], in1=bc16(gi), op=ALU.mult), **rwe)
            for g2 in range(2):
                ps_ = slice(64 * g2, 64 * g2 + 64)
                op("dve", lambda e, ps_=ps_, g2=g2, s4=s4: e.tensor_tensor(out=Epim[ps_, :, s4, g2, :], in0=e1[ps_, :, :], in1=e2[ps_, :, :],
                                                                           op=ALU.add), **rwe)
        op("dve", lambda e: e.tensor_copy(out=CAB[:, 0, 0, :], in_=PWre[:, :, MI(32)]), **rw)
        op("dve", lambda e: e.tensor_copy(out=CAB[:, 0, 1, :], in_=PWre[:, :, MI(32)]), **rw)
        op("dve", lambda e: e.tensor_copy(out=CAB[:, 1, 0, :], in_=nPWim[:, :, MI(32)]), **rw)
        op("dve", lambda e: e.tensor_copy(out=CAB[:, 1, 1, :], in_=PWim[:, :, MI(32)]), **rw)
        prm_all = [prm_b, pw_b, ep_b]
        chk("P")
        kb.barrier(full=True)
        ar.off = p_mark

        U = ar.alloc((16, 8, 128), BF16)
        U_b = [Buf("U%d" % i) for i in range(16)]
        m1_mark = ar.off
        xn_all = ar.alloc((KT, L), BF16)
        SBK = 256
        NSB = L // SBK
        xa_b = [Buf("xa%d" % i) for i in range(NSB)]
        Wu = ar.alloc((KT, 512), BF16)
        Wu_b = Buf("Wu")
        dma("sp", Wu[:, :, :], WBu[l].rearrange("(k p) c -> p k c", p=128), reads=[wu_bs[l]], writes=[Wu_b])
        h32m = [ar.alloc((KT, SBK), F32) for _ in range(2)]
        h32m_b = [Buf("h32m%d" % i) for i in range(2)]
        sqm = ar.alloc((KT, SBK), BF16)
        sqm_b = Buf("sqm")
        rsms = [ar.alloc((1, SBK), F32)[:, 0, :] for _ in range(2)]
        rsms_b = [Buf("rsm%d" % i) for i in range(2)]
        tmpm = ar.alloc((1, SBK), F32)[:, 0, :]
        tmpm_b = Buf("tmpm")
        uT = [ar.alloc((4, 512), BF16) for _ in range(2)]
        uT_b = [[Buf("uT%d_%d" % (i, q)) for q in range(4)] for i in range(2)]
        for k in range(KT):
            op("pool", lambda e, k=k: e.tensor_scalar(out=Wu[:, k, :], in0=Wu[:, k, :], scalar1=g1col[:, l, k:k + 1], scalar2=None, op0=ALU.mult),
               reads=[Wu_b, small_b], writes=[Wu_b])
        rstdT = ar.alloc((1, 32), F32)[:, 0, :]
        rstdT_b = Buf("rstdT")
        def m1_load(sbk):
            dma("sp", h32m[sbk % 2][:, :, :], hsrc_d[:, sbk * SBK:(sbk + 1) * SBK].rearrange("(k p) t -> p k t", p=128),
                reads=[hd_b[sbk // 2]], writes=[h32m_b[sbk % 2]])

        m1_load(0)
        for sbk in range(NSB):
            hm, hm_b = h32m[sbk % 2], h32m_b[sbk % 2]
            tsb = slice(sbk * SBK, (sbk + 1) * SBK)
            if sbk + 1 < NSB:
                m1_load(sbk + 1)
            op("act", lambda e: e.activation(out=sqm[:, :, :], in_=hm[:, :, :], func=AF.Square), reads=[hm_b], writes=[sqm_b])
            mm([lambda e, k=k: e.matmul(PS[4][:, 0:SBK], lhsT=ones_b[:, :], rhs=sqm[:, k, :], start=(k == 0), stop=(k == KT - 1)) for k in range(KT)],
               reads=[sqm_b, io_b], writes=[PSb[4]])
            op("act", lambda e: e.activation(out=tmpm, in_=PS[4][:, 0:SBK], func=AF.Sqrt, scale=1.0 / D, bias=EPS), reads=[PSb[4]], writes=[tmpm_b])
            rs_, rs_b_ = rsms[sbk % 2], rsms_b[sbk % 2]
            op("dve", lambda e, rs_=rs_: e.reciprocal(out=rs_, in_=tmpm), reads=[tmpm_b], writes=[rs_b_])
            dma("sp", rstdT[8 * sbk:8 * sbk + 8, :], rs_[0:1, :].rearrange("a (k q) -> a k q", q=32), reads=[rs_b_], writes=[rstdT_b], partial=True)
            if sbk % 2 == 0:
                op("dve", lambda e: e.tensor_copy(out=xn_all[:, :, tsb], in_=hm[:, :, :]), reads=[hm_b], writes=[xa_b[sbk]])
            else:
                op("act", lambda e: e.activation(out=xn_all[:, :, tsb], in_=hm[:, :, :], func=AF.Identity), reads=[hm_b], writes=[xa_b[sbk]])
        chk("M1a")
        for j in range(8):
            if j == 1:
                chk("M1b")
            ub, ub_b = uT[j % 2], uT_b[j % 2]
            for s4 in range(4):
                bank = s4 % 2
                pos = 4 * j + s4
                mm([lambda e, k=k, pos=pos, bank=bank: e.matmul(
                    PS[bank][:, :], lhsT=xn_all[:, k, :].rearrange("p (c q) -> p q c", q=32)[:, pos, :], rhs=Wu[:, k, :],
                    start=(k == 0), stop=(k == KT - 1)) for k in range(KT)],
                   reads=[Wu_b] + xa_b, writes=[PSb[bank]])
                if bank == 0:
                    op("act", lambda e, s4=s4, pos=pos: e.activation(out=ub[:, s4, :], in_=PS[0][:, :], func=AF.Identity, scale=rstdT[:, pos:pos + 1]),
                       reads=[PSb[0], rstdT_b], writes=[ub_b[s4]])
                else:
                    op("dve", lambda e, s4=s4, pos=pos: e.tensor_scalar(out=ub[:, s4, :], in0=PS[1][:, :], scalar1=rstdT[:, pos:pos + 1], scalar2=None, op0=ALU.mult),
                       reads=[PSb[1], rstdT_b], writes=[ub_b[s4]])
            if j == 0:
                chk("M1c")
            for g in range(4):
                tb = 2 + (g % 2)
                psh = PS[tb][:, 0:256].bitcast(BF16)
                mm([lambda e, pq=pq, s4=s4, g=g, psh=psh: e.transpose(psh[32 * s4:32 * s4 + 32, 128 * pq:128 * pq + 128],
                                                                     ub[:, s4, 32 * (4 * g + pq):32 * (4 * g + pq) + 32], ident_b[:, :],
                                                                     tile_position=(0, 32 * s4))
                    for pq in range(4) for s4 in range(4)],
                   reads=ub_b + [io_b], writes=[PSb[tb]])
                src = psh.rearrange("p (a k) -> p a k", a=4)
                dst = U[:, 4 * g:4 * g + 4, j, :]
                if g % 2 == 0:
                    op("act", lambda e, src=src, dst=dst: e.activation(out=dst, in_=src, func=AF.Identity), reads=[PSb[tb]], writes=U_b[4 * g:4 * g + 4], partial=True)
                else:
                    op("dve", lambda e, src=src, dst=dst: e.tensor_copy(out=dst, in_=src), reads=[PSb[tb]], writes=U_b[4 * g:4 * g + 4], partial=True)
        if "U" in dump_aps:
            for pi_ in range(16):
                dma("pool", dump_aps["U"][:, 1024 * pi_:1024 * pi_ + 1024], U[:, pi_, :, :].rearrange("p j k -> p (j k)"), reads=U_b)

        chk("M1")
        ar.off = m1_mark
        kb.barrier(full=True)
        H = ar.alloc((129, 3, 16), F32)
        H_b = Buf("H")
        Zs_b = H_b
        Hb = ar.alloc((2, 16, 128), BF16)
        Hb_b = Buf("Hb")
        TA = ar.alloc((2, 16), F32)
        TBt = ar.alloc((2, 16), F32)
        S1 = ar.alloc((2, 16), F32)
        sc_b = Buf("scan_tmp")
        NPB = 2
        Dg = [[ar.alloc((8, 128), BF16) for _ in range(3)] for _ in range(NPB)]
        Dg_b = [Buf("Dg%d" % i) for i in range(NPB)]
        Zw = [ar.alloc((2, 8, 128), BF16) for _ in range(NPB)]
        Zw_b = [Buf("Zw%d" % i) for i in range(NPB)]
        for pi in range(16):
            pb_ = pi % NPB
            for ci, tab in enumerate((PWre, PWim, nPWim)):
                op("pool", lambda e, ci=ci, tab=tab, pb_=pb_, pi=pi: e.tensor_tensor(
                    out=Dg[pb_][ci][:, :, :], in0=ident_f.unsqueeze(1).to_broadcast([128, 8, 128]),
                    in1=tab[:, pi, MI(0):MI(32):4].unsqueeze(2).to_broadcast([128, 8, 128]), op=ALU.mult),
                   reads=[cst_b, pw_b], writes=[Dg_b[pb_]], partial=(ci > 0))
            Dre_, Dim_, Dnim_ = Dg[pb_]
            lre = Epre[:, pi, :, :, :].rearrange("p s g c -> p (s g c)")
            lim = Epim[:, pi, :, :, :].rearrange("p s g c -> p (s g c)")
            for comp in range(2):
                r1, r2 = (Dre_, Dnim_) if comp == 0 else (Dim_, Dre_)
                for hlf in range(2):
                    bank = 2 * comp + hlf
                    js = slice(4 * hlf, 4 * hlf + 4)
                    mm([lambda e, r1=r1, js=js, bank=bank: e.matmul(PS[bank][:, :], lhsT=lre, rhs=r1[:, js, :].rearrange("p j c -> p (j c)"),
                                                                  start=True, stop=False),
                        lambda e, r2=r2, js=js, bank=bank: e.matmul(PS[bank][:, :], lhsT=lim, rhs=r2[:, js, :].rearrange("p j c -> p (j c)"),
                                                                  start=False, stop=True)],
                       reads=[ep_b, Dg_b[pb_]], writes=[PSb[bank]])
                    eng = "act" if hlf == 0 else "dve"
                    dst = Zw[pb_][:, comp, js, :].rearrange("p j c -> p (j c)")
                    if eng == "act":
                        op("act", lambda e, dst=dst, bank=bank: e.activation(out=dst, in_=PS[bank][:, :], func=AF.Identity),
                           reads=[PSb[bank]], writes=[Zw_b[pb_]], partial=(comp + hlf > 0))
                    else:
                        op("dve", lambda e, dst=dst, bank=bank: e.tensor_copy(out=dst, in_=PS[bank][:, :]),
                           reads=[PSb[bank]], writes=[Zw_b[pb_]], partial=True)
            zb = 4 + (pi % 2)
            for comp in range(2):
                mm([lambda e, comp=comp, j=j, pb_=pb_, pi=pi, zb=zb: e.matmul(
                    PS[zb][:, 128 * comp:128 * comp + 128], lhsT=Zw[pb_][:, comp, 7 - j, :], rhs=U[:, pi, j, :],
                    start=(j == 0), stop=(j == 7)) for j in range(8)],
                   reads=[Zw_b[pb_], U_b[pi]], writes=[PSb[zb]], partial=(comp == 1))
            op("dve" if pi % 2 else "act",
               (lambda e, zb=zb, pi=pi: e.tensor_copy(out=H[:, 1:129, 0:2, pi].rearrange("p k c -> p c k"),
                                                    in_=PS[zb][:, 0:256].rearrange("p (c k) -> p c k", c=2))) if pi % 2 else
               (lambda e, zb=zb, pi=pi: e.activation(out=H[:, 1:129, 0:2, pi].rearrange("p k c -> p c k"),
                                                   in_=PS[zb][:, 0:256].rearrange("p (c k) -> p c k", c=2), func=AF.Identity)),
               reads=[PSb[zb]], writes=[Zs_b], partial=True)
        chk("S5A")
        op("dve", lambda e: e.memset(H[:, 0, :, :], 0.0), writes=[H_b], partial=True)
        hz = [H_b, sc_b] + prm_all
        for k in range(128):
            op("dve", lambda e, k=k: e.tensor_tensor(out=TA[:, :, :], in0=CAB[:, 0, :, :], in1=H[:, k, 0:2, :], op=ALU.mult), reads=hz, writes=[sc_b])
            op("dve", lambda e, k=k: e.tensor_tensor(out=TBt[:, :, :], in0=CAB[:, 1, :, :], in1=H[:, k, 1:3, :], op=ALU.mult), reads=hz, writes=[sc_b])
            op("dve", lambda e, k=k: e.tensor_tensor(out=S1[:, :, :], in0=H[:, k + 1, 0:2, :], in1=TA[:, :, :], op=ALU.add), reads=hz, writes=[sc_b])
            op("dve", lambda e, k=k: e.tensor_tensor(out=H[:, k + 1, 0:2, :], in0=S1[:, :, :], in1=TBt[:, :, :], op=ALU.add), reads=hz, writes=[H_b])
            op("dve", lambda e, k=k: e.tensor_copy(out=H[:, k + 1, 2, :], in_=H[:, k + 1, 0, :]), reads=hz, writes=[H_b])
        for comp in range(2):
            op("act", lambda e, comp=comp: e.activation(out=Hb[:, comp, :, :], in_=H[:, 0:128, comp, :].rearrange("p k a -> p a k"), func=AF.Identity),
               reads=[H_b], writes=[Hb_b], partial=(comp == 1))
        chk("S5S")
        FRE = [ar.alloc((36, 2, 16), BF16) for _ in range(NPB)]
        FIM = [ar.alloc((36, 2, 16), BF16) for _ in range(NPB)]
        F_b = [Buf("F%d" % i) for i in range(NPB)]
        TAB = [ar.alloc((32, 32), BF16) for _ in range(NPB)]
        TAB_b = [Buf("TAB%d" % i) for i in range(NPB)]
        Dd = [ar.alloc((1, 128), BF16) for _ in range(NPB)]
        Dd_b = [Buf("Dd%d" % i) for i in range(NPB)]
        fts = [[ar.alloc((36, 16), F32) for _ in range(4)] for _ in range(2)]
        ft_bs = [Buf("ft0"), Buf("ft1")]
        Fz_b = [Buf("Fz%d" % i) for i in range(NPB)]
        for i in range(NPB):
            op("pool", lambda e, i=i: e.memset(FRE[i][:, :, :, :], 0.0), writes=[F_b[i], Fz_b[i]])
            op("pool", lambda e, i=i: e.memset(FIM[i][:, :, :, :], 0.0), writes=[F_b[i], Fz_b[i]], partial=True)
        for pi in range(16):
            pb_ = pi % NPB
            bcn = lambda a: a.unsqueeze(1).to_broadcast([128, 36, 16])
            bcc = lambda a: a.unsqueeze(2).to_broadcast([128, 36, 16])
            ctr, cti = CTre[:, pi, :], CTim[:, pi, :]
            pwr, pwi, npwi = PWre[:, pi, 0:36], PWim[:, pi, 0:36], nPWim[:, pi, 0:36]
            fe = "pool" if pi % 2 == 0 else "dve"
            ft = fts[pi % 2]
            ft_b = ft_bs[pi % 2]
            rwf = dict(reads=prm_all + [ft_b], writes=[ft_b])
            op(fe, lambda e: e.tensor_tensor(out=ft[0][:, :, :], in0=bcn(ctr), in1=bcc(pwr), op=ALU.mult), **rwf)
            op(fe, lambda e: e.tensor_tensor(out=ft[1][:, :, :], in0=bcn(cti), in1=bcc(pwi), op=ALU.mult), **rwf)
            op(fe, lambda e: e.tensor_tensor(out=ft[2][:, :, :], in0=bcn(ctr), in1=bcc(npwi), op=ALU.mult), **rwf)
            op(fe, lambda e: e.tensor_tensor(out=ft[3][:, :, :], in0=bcn(cti), in1=bcc(pwr), op=ALU.mult), **rwf)
            for g2 in range(2):
                ps_ = slice(64 * g2, 64 * g2 + 64)
                op(fe, lambda e, ps_=ps_, g2=g2, pb_=pb_: e.tensor_tensor(out=FRE[pb_][ps_, :, g2, :], in0=ft[0][ps_, :, :], in1=ft[1][ps_, :, :],
                                                                        op=ALU.subtract), reads=[ft_b, Fz_b[pb_]], writes=[F_b[pb_]], partial=True)
                op(fe, lambda e, ps_=ps_, g2=g2, pb_=pb_: e.tensor_tensor(out=FIM[pb_][ps_, :, g2, :], in0=ft[2][ps_, :, :], in1=ft[3][ps_, :, :],
                                                                        op=ALU.subtract), reads=[ft_b, Fz_b[pb_]], writes=[F_b[pb_]], partial=True)
            op("pool", lambda e, pb_=pb_, pi=pi: e.tensor_scalar(out=Dd[pb_][:, 0, :], in0=ident_f, scalar1=Dcol[:, 0, pi:pi + 1], scalar2=None, op0=ALU.mult),
               reads=[cst_b, prm_b], writes=[Dd_b[pb_]])
            lre = Epre[:, pi, :, :, :].rearrange("p s g c -> p (s g c)")
            lim = Epim[:, pi, :, :, :].rearrange("p s g c -> p (s g c)")
            for hlf in range(2):
                bank = hlf
                ns = slice(16 * hlf, 16 * hlf + 16)
                ms = [lambda e, ns=ns, bank=bank, pb_=pb_: e.matmul(PS[bank][:, :], lhsT=lre, rhs=FRE[pb_][:, ns, :, :].rearrange("p n g c -> p (n g c)"),
                                                                  start=True, stop=False),
                      lambda e, ns=ns, bank=bank, pb_=pb_, hlf=hlf: e.matmul(PS[bank][:, :], lhsT=lim, rhs=FIM[pb_][:, ns, :, :].rearrange("p n g c -> p (n g c)"),
                                                                           start=False, stop=(hlf == 1))]
                if hlf == 0:
                    ms.append(lambda e, pb_=pb_: e.matmul(PS[0][:, 0:128], lhsT=ident_b[:, :], rhs=Dd[pb_][:, 0, :], start=False, stop=True))
                mm(ms, reads=[ep_b, F_b[pb_], Dd_b[pb_], small_b, io_b], writes=[PSb[bank]])
                op("dve", lambda e, bank=bank, pb_=pb_, hlf=hlf: e.tensor_tensor(
                    out=TAB[pb_][:, 16 * hlf:16 * hlf + 16, :].rearrange("p m c -> p (m c)"), in0=PS[bank][:, :],
                    in1=tabmask[:, 512 * hlf:512 * hlf + 512], op=ALU.mult),
                   reads=[PSb[bank], cst_b], writes=[TAB_b[pb_]], partial=(hlf == 1))
            for ib in range(2):
                bank = 2 + 2 * (pi % 2) + ib
                grp = []
                for i4 in range(4):
                    i = 4 * ib + i4
                    o_ = PS[bank][:, 128 * i4:128 * i4 + 128]
                    for j in range(i + 1):
                        dlt = 4 * (i - j)
                        grp.append(lambda e, o_=o_, dlt=dlt, j=j, pb_=pb_, pi=pi: e.matmul(
                            o_, lhsT=TAB[pb_][:, dlt:dlt + 4, :].rearrange("p m c -> p (m c)"), rhs=U[:, pi, j, :], start=(j == 0), stop=False))
                    grp.append(lambda e, o_=o_, i=i, pb_=pb_, pi=pi: e.matmul(
                        o_, lhsT=FRE[pb_][:, 4 * i + 4:4 * i + 8, :, :].rearrange("p n g c -> p (n g c)"), rhs=Hb[:, 0, pi, :], start=False, stop=False))
                    grp.append(lambda e, o_=o_, i=i, pb_=pb_, pi=pi: e.matmul(
                        o_, lhsT=FIM[pb_][:, 4 * i + 4:4 * i + 8, :, :].rearrange("p n g c -> p (n g c)"), rhs=Hb[:, 1, pi, :], start=False, stop=True))
                mm(grp, reads=[TAB_b[pb_], F_b[pb_], U_b[pi], Hb_b], writes=[PSb[bank]])
                for t4 in range(4):
                    src = PS[bank][32 * t4:32 * t4 + 32, :].rearrange("p (i k) -> p i k", i=4)
                    q = pi % 4
                    dst = yfm[32 * q:32 * q + 32, pi // 4, :].rearrange("p (i t k) -> p i t k", i=8, t=4)[:, 4 * ib:4 * ib + 4, t4, :]
                    op("dve", lambda e, src=src, dst=dst: e.tensor_copy(out=dst, in_=src),
                       reads=[PSb[bank]], writes=[yfm_b[pi // 4]], partial=True)
        if "yfm" in dump_aps:
            for ct in range(4):
                dma("pool", dump_aps["yfm"][128 * ct:128 * ct + 128, :], yfm[:, ct, :], reads=[yfm_b[ct]])

        chk("S5")
        kb.barrier(full=True)
        ar.reset()
        ring["t"] = [ar.alloc((1, SLOT_BYTES // 2), BF16) for _ in range(NSLOT)]
        ring["b"] = [Buf("ring%d" % i) for i in range(NSLOT)]
        ring["wc"] = wcast_bs[l]
        h32s = [ar.alloc((KT, TB), F32) for _ in range(2)]
        h32s_b = [Buf("h32_%d" % i) for i in range(2)]
        xns = [ar.alloc((KT, TB), BF16) for _ in range(2)]
        xns_b = [[Buf("xn%d_%d" % (i, k)) for k in range(KT)] for i in range(2)]
        ygla = ar.alloc((KT, TB), BF16)
        ygla_b = Buf("ygla")
        sqt, sq_b = ygla, ygla_b
        rsN = ar.alloc((1, TB), F32)[:, 0, :]
        rsN_b = Buf("rsN")
        tmpN = ar.alloc((1, TB), F32)[:, 0, :]
        tmpN_b = Buf("tmpN")
        sq2 = ar.alloc((2, TB), BF16)
        sq2_b = Buf("sq2")
        ygl = ar.alloc((4, TB), BF16)
        ygl_b = Buf("ygl")
        ys5 = ar.alloc((4, TB), BF16)
        ys5_b = Buf("ys5")
        wupa = ar.alloc((1, 512), BF16, nparts=32)
        wupa_b = Buf("wupa")
        wup32 = ar.alloc((1, 512), F32, nparts=32)
        rs2 = ar.alloc((1, TB), F32)[:, 0, :]
        rs2_b = Buf("rs2")
        tmpA = ar.alloc((1, TB), F32)[:, 0, :]
        tmpA_b = Buf("tmpA")
        tmpB = ar.alloc((1, TB), F32)[:, 0, :]
        tmpB_b = Buf("tmpB")
        tbf = [ar.alloc((1, TB), BF16)[:, 0, :] for _ in range(2)]
        tbf_b = [Buf("tbf%d" % i) for i in range(2)]
        dec = ar.alloc((1, 32), F32)[:, 0, :]
        dec_b = Buf("dec")
        S_bf = ar.alloc((8, 256), BF16)
        Sbf_b = [Buf("Sbf%d" % i) for i in range(8)]
        PB = Buf("phase")
        gb = lambda n: Buf(n, guard=PB)
        r0 = ar.off
        sg5 = ar.alloc((KT, TB), BF16)
        sgg = ar.alloc((KT, TB), BF16)
        mixed = ar.alloc((KT, TB), BF16)
        hid = ar.alloc((FT, TB), BF16)
        r1 = ar.off
        ar.off = r0
        q_fm = ar.alloc((4, TB), BF16)
        kendT = ar.alloc((4, 512), BF16)
        vT = ar.alloc((4, 1024), BF16)
        Eexp = ar.alloc((4, 512), F32)
        o_sb = ar.alloc((8, TB), F32)
        sp32 = [ar.alloc((1, 512), F32)[:, 0, :] for _ in range(2)]
        e32 = ar.alloc((1, 512), F32)[:, 0, :]
        assert ar.off <= r1, (ar.off, r1)
        ar.off = r1
        sg5_b, sgg_b, mixed_b, hid_b = gb("sg5"), gb("sgg"), gb("mixed"), gb("hid")
        q_b = gb("q")
        sp32_b = [gb("sp%d" % i) for i in range(2)]
        e32_b = gb("e32")
        Eexp_b = [gb("Eexp%d" % i) for i in range(4)]
        kend_b = [gb("kend%d" % i) for i in range(4)]
        vT_b = [gb("vT%d" % i) for i in range(4)]
        o_b = gb("o")

        wupz_b = Buf("wupz")
        op("dve", lambda e: e.memset(wup32[:, 0, :], 0.0), writes=[wupz_b, wupa_b])
        dma("sp", wup32[0:16, 0, :], P["gla_w_gate_up"][l], writes=[wupa_b], partial=True, reads=[wupz_b])
        dma("sp", wup32[16:17, 0, :], P["gla_b_gate"][l].rearrange("(a c) -> a c", a=1), writes=[wupa_b], partial=True, reads=[wupz_b])
        op("dve", lambda e: e.tensor_copy(out=wupa[:, 0, :], in_=wup32[:, 0, :]), reads=[wupa_b], writes=[wupa_b])
        for h in range(4):
            op("dve", lambda e, h=h: e.memset(Sst[:, h, :], 0.0), writes=[Sst_b[h]])

        Wl = {n: WB[n][l] for n in WEIGHT_NAMES}

        def fm_tiles(wsrc, kt_n, col0, ncols, rhs_fn, consume, chunk=512, rbufs=()):
            m = 0
            for c0 in range(0, ncols, chunk):
                cw = min(chunk, ncols - c0)
                wap, wb = wload(wsrc[:, col0 + c0:col0 + c0 + cw], kt_n, cw)
                for t in range(0, cw, 128):
                    tw = min(128, cw - t)
                    bank = m % 2
                    mm([lambda e, k=k, t=t, tw=tw, bank=bank: e.matmul(PS[bank][0:tw, :], lhsT=wap[:, k, t:t + tw], rhs=rhs_fn(k),
                                                                      start=(k == 0), stop=(k == kt_n - 1)) for k in range(kt_n)],
                       reads=[wb] + list(rbufs), writes=[PSb[bank]])
                    consume(m, bank)
                    m += 1

        if not last:
            cast_q = [lambda: cast_wu(l + 1)] + cast_pieces(l + 1, 512)
        n_per = (len(cast_q) + 4 * NBLK - 1) // (4 * NBLK)

        def cast_some(n):
            for _ in range(n):
                if cast_q:
                    cast_q.pop(0)()

        for b in range(NBLK):
            tsl = slice(b * TB, (b + 1) * TB)
            h32, h32_b = h32s[b % 2], h32s_b[b % 2]
            xn, xn_b = xns[b % 2], xns_b[b % 2]
            if b == 0:
                dma("pool", h32[:, :, :], hsrc_d[:, tsl].rearrange("(k p) t -> p k t", p=128), reads=[hd_b[b]], writes=[h32_b])

            if b == 0:
                rms_block(h32, h32_b, g1col[:, l, :], sqt, sq_b, 6, rsN, rsN_b, xn, xn_b, tmpN, tmpN_b)
            xn_rhs = lambda k: xn[:, k, :]
            fm_tiles(Wl["w_in"], KT, O_Q, 512, xn_rhs,
                     lambda m, bank: op("act", lambda e: e.activation(out=q_fm[:, m, :], in_=PS[bank][:, :], func=AF.Identity, scale=128 ** -0.5),
                                        reads=[PSb[bank]], writes=[q_b] + ([PB] if m == 0 else []), partial=True), rbufs=xn_b)
            fm_tiles(Wl["w_in"], KT, O_A, 16, xn_rhs,
                     lambda m, bank: op("act", lambda e: e.activation(out=alow[0:16, :], in_=PS[bank][0:16, :], func=AF.Identity),
                                        reads=[PSb[bank], alowz_b], writes=[alow_b], partial=True), rbufs=xn_b)
            for tt in range(4):
                tsl128 = slice(128 * tt, 128 * tt + 128)
                sp_, spb = sp32[tt % 2], sp32_b[tt % 2]
                mm([lambda e: e.matmul(PS[2][:, :], lhsT=alow[0:17, tsl128], rhs=wupa[0:17, 0, :], start=True, stop=True)],
                   reads=[alow_b, wupa_b], writes=[PSb[2]])
                op("act", lambda e: e.activation(out=e32, in_=PS[2][:, :], func=AF.Exp, scale=-1.0), reads=[PSb[2]], writes=[e32_b])
                op("act", lambda e, sp_=sp_: e.activation(out=sp_, in_=e32, func=AF.Ln, bias=1.0), reads=[e32_b], writes=[spb])
                mm([lambda e, sp_=sp_: e.matmul(PS[3][:, :], lhsT=mrev, rhs=sp_, start=True, stop=True)], reads=[spb, cst_b], writes=[PSb[3]])
                op("act", lambda e, tt=tt: e.activation(out=Eexp[:, tt, :], in_=PS[3][:, :], func=AF.Exp, scale=-1.0 / 16.0),
                   reads=[PSb[3]], writes=[Eexp_b[tt]])
                mm([lambda e, h=h, sp_=sp_, tt=tt: e.matmul(PS[7][:, 8 * h + 2 * tt:8 * h + 2 * tt + 2], lhsT=sp_[:, 128 * h:128 * h + 128], rhs=cind,
                                                           start=True, stop=True) for h in range(4)],
                   reads=[spb, cst_b], writes=[PSb[7]], partial=(tt > 0))
            op("act", lambda e: e.activation(out=dec, in_=PS[7][:, 0:32], func=AF.Exp, scale=-1.0 / 16.0), reads=[PSb[7]], writes=[dec_b])
            wk, wk_b = wload(Wl["w_in"][:, O_K:O_K + 512], KT, 512)
            for tt in range(4):
                bank = tt % 2
                mm([lambda e, k=k, tt=tt, bank=bank: e.matmul(PS[bank][:, :], lhsT=xn[:, k, 128 * tt:128 * tt + 128], rhs=wk[:, k, :],
                                                            start=(k == 0), stop=(k == KT - 1)) for k in range(KT)],
                   reads=[wk_b] + xn_b, writes=[PSb[bank]])
                op("dve", lambda e, tt=tt, bank=bank: e.tensor_tensor(out=kendT[:, tt, :], in0=PS[bank][:, :], in1=Eexp[:, tt, :], op=ALU.mult),
                   reads=[PSb[bank], Eexp_b[tt]], writes=[kend_b[tt]])
            for k in range(4):
                op("act", lambda e, k=k: e.activation(out=ygl[:, k, :].rearrange("p (c q) -> p c q", q=32),
                                                    in_=yfm[:, k, :].rearrange("p (q c) -> p c q", q=32)[:, 16 * b:16 * b + 16, :],
                                                    func=AF.Gelu_apprx_tanh), reads=yfm_b, writes=[ygl_b], partial=(k > 0))
            for vc in range(2):
                wv, wv_b = wload(Wl["w_in"][:, O_V + 512 * vc:O_V + 512 * vc + 512], KT, 512)
                for tt in range(4):
                    bank = tt % 2
                    mm([lambda e, k=k, tt=tt, bank=bank: e.matmul(PS[bank][:, :], lhsT=xn[:, k, 128 * tt:128 * tt + 128], rhs=wv[:, k, :],
                                                                start=(k == 0), stop=(k == KT - 1)) for k in range(KT)],
                       reads=[wv_b] + xn_b, writes=[PSb[bank]])
                    op("dve", lambda e, tt=tt, bank=bank, vc=vc: e.tensor_copy(out=vT[:, tt, 512 * vc:512 * vc + 512], in_=PS[bank][:, :]),
                       reads=[PSb[bank]], writes=[vT_b[tt]], partial=(vc == 1))
            def gla_o(c):
                obank = 4 + (c % 2)
                sb_ = (c % 2) * 4
                mm([lambda e, h=h, dv=dv, obank=obank, c=c, sb_=sb_: e.matmul(PS[obank][:, 64 * (2 * h + dv):64 * (2 * h + dv) + 64],
                                                                          lhsT=S_bf[:, sb_ + h, 128 * dv:128 * dv + 128], rhs=q_fm[:, h, 64 * c:64 * c + 64],
                                                                          start=True, stop=True) for h in range(4) for dv in range(2)],
                   reads=Sbf_b[sb_:sb_ + 4] + [q_b], writes=[PSb[obank]])
                op("act", lambda e, obank=obank, c=c: e.activation(out=o_sb[:, :, 64 * c:64 * c + 64], in_=PS[obank][:, :].rearrange("p (a t) -> p a t", a=8),
                                                                 func=AF.Identity), reads=[PSb[obank]], writes=[o_b], partial=(c > 0))

            for c in range(8):
                tt, hf = c // 2, c % 2
                prt = slice(64 * hf, 64 * hf + 64)
                sb_ = (c % 2) * 4
                for hp in range(2):
                    bank = 2 + hp
                    mm([lambda e, h=h, bank=bank: e.matmul(PS[bank][:, 256 * (h % 2):256 * (h % 2) + 256], lhsT=kendT[prt, tt, 128 * h:128 * h + 128],
                                                         rhs=vT[prt, tt, 256 * h:256 * h + 256], start=True, stop=True) for h in (2 * hp, 2 * hp + 1)],
                       reads=[kend_b[tt], vT_b[tt]], writes=[PSb[bank]])
                    for h in (2 * hp, 2 * hp + 1):
                        op("dve", lambda e, h=h, bank=bank, c=c: e.scalar_tensor_tensor(
                            out=Sst[:, h, :], in0=Sst[:, h, :], scalar=dec[:, 8 * h + c:8 * h + c + 1],
                            in1=PS[bank][:, 256 * (h % 2):256 * (h % 2) + 256], op0=ALU.mult, op1=ALU.add),
                           reads=[PSb[bank], dec_b, Sst_b[h]], writes=[Sst_b[h]])
                        op("act", lambda e, h=h, sb_=sb_: e.activation(out=S_bf[:, sb_ + h, :], in_=Sst[:, h, :], func=AF.Identity),
                           reads=[Sst_b[h]], writes=[Sbf_b[sb_ + h]])
                if c >= 1:
                    gla_o(c - 1)
            gla_o(7)
            for h in range(4):
                op("act", lambda e, h=h: e.activation(out=sq2[:, :, :], in_=o_sb[:, 2 * h:2 * h + 2, :], func=AF.Square), reads=[o_b], writes=[sq2_b])
                mm([lambda e, dv=dv: e.matmul(PS[6][:, :], lhsT=ones_b[:, :], rhs=sq2[:, dv, :], start=(dv == 0), stop=(dv == 1)) for dv in range(2)],
                   reads=[sq2_b, small_b, io_b], writes=[PSb[6]])
                op("act", lambda e: e.activation(out=tmpA, in_=PS[6][:, :], func=AF.Sqrt, scale=1.0 / 256, bias=EPS), reads=[PSb[6]], writes=[tmpA_b])
                op("dve", lambda e: e.reciprocal(out=rs2, in_=tmpA), reads=[tmpA_b], writes=[rs2_b])

                def g_consume(m, bank, h=h):
                    i = 2 * h + m
                    tb, tbb = tbf[m % 2], tbf_b[m % 2]
                    op("act", lambda e: e.activation(out=tb, in_=PS[bank][:, :], func=AF.Silu), reads=[PSb[bank]], writes=[tbb])
                    op("dve", lambda e: e.scalar_tensor_tensor(out=tmpB, in0=o_sb[:, i, :], scalar=hngcol[:, l, i:i + 1], in1=rs2,
                                                               op0=ALU.mult, op1=ALU.mult), reads=[o_b, rs2_b, small_b], writes=[tmpB_b])
                    op("dve", lambda e: e.tensor_tensor(out=ygla[:, i, :], in0=tmpB, in1=tb, op=ALU.mult), reads=[tmpB_b, tbb], writes=[ygla_b],
                       partial=(i > 0))
                fm_tiles(Wl["w_in"], KT, O_G + 256 * h, 256, xn_rhs, g_consume, chunk=256, rbufs=xn_b)
            if b + 1 < NBLK:
                dma("pool", h32s[(b + 1) % 2][:, :, :], hsrc_d[:, (b + 1) * TB:(b + 2) * TB].rearrange("(k p) t -> p k t", p=128),
                    reads=[hd_b[b + 1]], writes=[h32s_b[(b + 1) % 2]])
            cast_some(n_per)
            yrhs = lambda k: ygl[:, k, :]
            wg0, wg0_b = wload(Wl["s5_w_glu"][:, 0:512], 4, 512)
            wg1, wg1_b = wload(Wl["s5_w_glu"][:, 512:1024], 4, 512)
            for m in range(4):
                mm([lambda e, k=k, m=m: e.matmul(PS[0][:, :], lhsT=wg0[:, k, 128 * m:128 * m + 128], rhs=yrhs(k), start=(k == 0), stop=(k == 3)) for k in range(4)],
                   reads=[wg0_b, ygl_b], writes=[PSb[0]])
                mm([lambda e, k=k, m=m: e.matmul(PS[1][:, :], lhsT=wg1[:, k, 128 * m:128 * m + 128], rhs=yrhs(k), start=(k == 0), stop=(k == 3)) for k in range(4)],
                   reads=[wg1_b, ygl_b], writes=[PSb[1]])
                op("act", lambda e: e.activation(out=tmpA, in_=PS[1][:, :], func=AF.Sigmoid), reads=[PSb[1]], writes=[tmpA_b])
                op("dve", lambda e, m=m: e.tensor_tensor(out=ys5[:, m, :], in0=PS[0][:, :], in1=tmpA, op=ALU.mult), reads=[PSb[0], tmpA_b], writes=[ys5_b],
                   partial=(m > 0))
            fm_tiles(Wl["w_in"], KT, O_GS5, 1024, xn_rhs,
                     lambda m, bank: op("act", lambda e: e.activation(out=sg5[:, m, :], in_=PS[bank][:, :], func=AF.Sigmoid),
                                        reads=[PSb[bank]], writes=[sg5_b] + ([PB] if m == 0 else []), partial=(m > 0)), rbufs=xn_b)
            fm_tiles(Wl["w_in"], KT, O_GG, 1024, xn_rhs,
                     lambda m, bank: op("act", lambda e: e.activation(out=sgg[:, m, :], in_=PS[bank][:, :], func=AF.Sigmoid),
                                        reads=[PSb[bank]], writes=[sgg_b], partial=(m > 0)), rbufs=xn_b)
            for c2 in range(2):
                ws, ws_b = wload(Wl["w_branch_s5"][:, 512 * c2:512 * c2 + 512], 4, 512)
                wgl, wgl_b = wload(Wl["w_branch_gla"][:, 512 * c2:512 * c2 + 512], 8, 512)
                for t in range(4):
                    m = 4 * c2 + t
                    mm([lambda e, k=k, t=t: e.matmul(PS[0][:, :], lhsT=ws[:, k, 128 * t:128 * t + 128], rhs=ys5[:, k, :], start=(k == 0), stop=(k == 3)) for k in range(4)],
                       reads=[ws_b, ys5_b], writes=[PSb[0]])
                    mm([lambda e, k=k, t=t: e.matmul(PS[1][:, :], lhsT=wgl[:, k, 128 * t:128 * t + 128], rhs=ygla[:, k, :], start=(k == 0), stop=(k == 7)) for k in range(8)],
                       reads=[wgl_b, ygla_b], writes=[PSb[1]])
                    op("dve", lambda e, m=m: e.tensor_tensor(out=tmpA, in0=PS[0][:, :], in1=sg5[:, m, :], op=ALU.mult), reads=[PSb[0], sg5_b], writes=[tmpA_b])
                    op("dve", lambda e, m=m: e.tensor_tensor(out=tmpB, in0=PS[1][:, :], in1=sgg[:, m, :], op=ALU.mult), reads=[PSb[1], sgg_b], writes=[tmpB_b])
                    op("dve", lambda e, m=m: e.tensor_tensor(out=mixed[:, m, :], in0=tmpA, in1=tmpB, op=ALU.add), reads=[tmpA_b, tmpB_b], writes=[mixed_b],
                       partial=(m > 0))
            cast_some(n_per)
            cast_some(n_per)
            hk_b = [Buf("hk%d" % m) for m in range(KT)]
            def out_consume(m, bank):
                op("dve", lambda e: e.tensor_tensor(out=h32[:, m, :], in0=PS[bank][:, :], in1=h32[:, m, :], op=ALU.add),
                   reads=[PSb[bank], h32_b], writes=[h32_b, hk_b[m]], partial=True)
                op("act", lambda e: e.activation(out=sqt[:, m, :], in_=h32[:, m, :], func=AF.Square), reads=[hk_b[m]], writes=[sq_b], partial=(m > 0))
                if m >= 1:
                    ssq_mm(m - 1)

            def ssq_mm(m):
                mm([lambda e: e.matmul(PS[6][:, :], lhsT=ones_b[:, :], rhs=sqt[:, m, :], start=(m == 0), stop=(m == KT - 1))],
                   reads=[sq_b, small_b, io_b], writes=[PSb[6]], partial=(m > 0))
            fm_tiles(Wl["w_out"], KT, 0, 1024, lambda k: mixed[:, k, :], out_consume, rbufs=[mixed_b])
            ssq_mm(KT - 1)
            if "hmix" in dump_aps:
                dma("pool", dump_aps["hmix"][:, tsl].rearrange("(k p) t -> p k t", p=128), h32[:, :, :], reads=[h32_b])
            op("act", lambda e: e.activation(out=tmpA, in_=PS[6][:, :], func=AF.Sqrt, scale=1.0 / D, bias=EPS), reads=[PSb[6]], writes=[tmpA_b])
            op("dve", lambda e: e.reciprocal(out=rs2, in_=tmpA), reads=[tmpA_b], writes=[rs2_b])
            make_xn(h32, h32_b, g2col[:, l, :], rs2, rs2_b, xn, xn_b)
            for c0 in range(0, DFF, 256):
                if c0 == 1280:
                    cast_some(n_per)
                wga, wga_b = wload(Wl["w_ffn_gate"][:, c0:c0 + 256], KT, 256)
                wua, wua_b = wload(Wl["w_ffn_up"][:, c0:c0 + 256], KT, 256)
                for t in range(2):
                    j = c0 // 128 + t
                    if j == 0:
                        for k in range(KT):
                            mm([lambda e, k=k, t=t: e.matmul(PS[0][:, :], lhsT=wga[:, k, 0:128], rhs=xn[:, k, :], start=(k == 0), stop=(k == KT - 1))],
                               reads=[wga_b, xn_b[k]], writes=[PSb[0]], partial=(k > 0))
                    else:
                        mm([lambda e, k=k, t=t: e.matmul(PS[0 + 2 * t][:, :], lhsT=wga[:, k, 128 * t:128 * t + 128], rhs=xn[:, k, :], start=(k == 0), stop=(k == KT - 1))
                            for k in range(KT)], reads=[wga_b] + xn_b, writes=[PSb[0 + 2 * t]])
                    mm([lambda e, k=k, t=t: e.matmul(PS[1 + 2 * t][:, :], lhsT=wua[:, k, 128 * t:128 * t + 128], rhs=xn[:, k, :], start=(k == 0), stop=(k == KT - 1))
                        for k in range(KT)], reads=[wua_b] + xn_b, writes=[PSb[1 + 2 * t]])
                    tb, tbb = tbf[t], tbf_b[t]
                    op("act", lambda e, t=t, tb=tb: e.activation(out=tb, in_=PS[0 + 2 * t][:, :], func=AF.Silu), reads=[PSb[0 + 2 * t]], writes=[tbb])
                    op("dve", lambda e, t=t, tb=tb, j=j: e.tensor_tensor(out=hid[:, j, :], in0=PS[1 + 2 * t][:, :], in1=tb, op=ALU.mult),
                       reads=[PSb[1 + 2 * t], tbb], writes=[hid_b], partial=(j > 0))
            cast_some(n_per)
            if b + 1 < NBLK:
                rms_block(h32s[(b + 1) % 2], h32s_b[(b + 1) % 2], g1col[:, l, :], sqt, sq_b, 6, rsN, rsN_b,
                          xns[(b + 1) % 2], xns_b[(b + 1) % 2], tmpN, tmpN_b)
            for m2 in range(4):
                wd0, wd0_b = wload(Wl["w_ffn_down"][0:1408, 256 * m2:256 * m2 + 256], 11, 256)
                wd1, wd1_b = wload(Wl["w_ffn_down"][1408:2816, 256 * m2:256 * m2 + 256], 11, 256)
                for t in range(2):
                    m = 2 * m2 + t
                    bank = 4 + t
                    mm([lambda e, k=k, t=t, bank=bank: e.matmul(PS[bank][:, :], lhsT=(wd0 if k < 11 else wd1)[:, k % 11, 128 * t:128 * t + 128], rhs=hid[:, k, :],
                                                              start=(k == 0), stop=(k == FT - 1)) for k in range(FT)],
                       reads=[wd0_b, wd1_b, hid_b], writes=[PSb[bank]])
                    op("dve", lambda e, m=m, bank=bank: e.tensor_tensor(out=h32[:, m, :], in0=PS[bank][:, :], in1=h32[:, m, :], op=ALU.add),
                       reads=[PSb[bank], h32_b], writes=[h32_b])
            if last and final_norm:
                rms_block(h32, h32_b, gfcol, sqt, sq_b, 6, rs2, rs2_b, None, None, tmpA, tmpA_b)
                for k in range(KT):
                    op("dve", lambda e, k=k: e.scalar_tensor_tensor(out=h32[:, k, :], in0=h32[:, k, :], scalar=gfcol[:, k:k + 1], in1=rs2,
                                                                  op0=ALU.mult, op1=ALU.mult), reads=[h32_b, rs2_b, small_b], writes=[h32_b])
            dst_d = outT if last else hT
            dma("pool", dst_d[:, tsl].rearrange("(k p) t -> p k t", p=128), h32[:, :, :], reads=[h32_b], writes=[hd_b[b]])
            if b == NBLK - 1:
                cast_some(len(cast_q))

    except _Stop:
        pass
    kb.barrier(full=True)
    es.close()
    return nc, kb


_CACHE = {}


def kernel(**inputs):
    x = np.asarray(inputs["x"], dtype=np.float32)
    B = x.shape[0]
    if "nc" not in _CACHE:
        _CACHE["nc"] = build_program()[0]
    nc = _CACHE["nc"]
    shared = {k: np.ascontiguousarray(np.asarray(v, dtype=np.float32)) for k, v in inputs.items() if k != "x"}
    shared["consts"] = CONSTS_NP
    in_maps = []
    for b in range(B):
        m = dict(shared)
        m["xT"] = np.ascontiguousarray(x[b].T)
        in_maps.append(m)
    res = run_bass_kernel_spmd(nc, in_maps, core_ids=list(range(B)))
    out = np.stack([np.ascontiguousarray(res.results[b]["outT"].T) for b in range(B)], axis=0)
    return out.astype(np.float32)
```

```python
import contextlib
import numpy as np
import concourse.bass as bass
import concourse.mybir as mybir
from concourse.bass_utils import run_bass_kernel_spmd

F32 = mybir.dt.float32
BF16 = mybir.dt.bfloat16
I32 = mybir.dt.int32
U8 = mybir.dt.uint8
AF = mybir.ActivationFunctionType
ALU = mybir.AluOpType

D = 1024
L = 4096
DEPTH = 4
KT = 8
TB = 512
NBLK = L // TB
DFF = 2816
FT = DFF // 128
DIN = 5648
EPS = 1e-6
O_U, O_Q, O_K, O_V, O_G, O_A, O_GS5, O_GG = 0, 512, 1024, 1536, 2560, 3584, 3600, 4624
NM = 40
TWO_PI = 6.28318
SAME_SYNC = True

WEIGHT_NAMES = ["w_in", "s5_w_glu", "w_branch_s5", "w_branch_gla", "w_out", "w_ffn_gate", "w_ffn_up", "w_ffn_down"]
WSHAPES = {"w_in": (D, DIN), "s5_w_glu": (512, 1024), "w_branch_s5": (512, 1024), "w_branch_gla": (1024, 1024),
           "w_out": (1024, 1024), "w_ffn_gate": (D, DFF), "w_ffn_up": (D, DFF), "w_ffn_down": (DFF, D)}


class Buf:
    __slots__ = ("name", "w", "r", "excl", "guard")

    def __init__(self, name="", excl=False, guard=None):
        self.name = name
        self.w = {}
        self.r = {}
        self.excl = excl
        self.guard = guard


def _with_guards(reads, writes):
    extra = []
    for b in list(reads) + list(writes):
        g = b.guard
        if g is not None and g not in writes and g not in extra:
            extra.append(g)
    return list(reads) + extra


class Eng:
    def __init__(self, name, obj, sem, same_sync):
        self.name, self.obj, self.sem, self.same_sync = name, obj, sem, same_sync
        self.count = 0
        self.known = {}


class DmaQ:
    def __init__(self, eng, sems):
        self.eng, self.sems = eng, sems
        self.n = 0


class KB:
    def __init__(self, nc, es):
        self.nc = nc
        mk = lambda n: es.enter_context(nc.semaphore(n))
        self.eng = {
            "pe": Eng("pe", nc.tensor, mk("s_pe"), False),
            "act": Eng("act", nc.scalar, mk("s_act"), SAME_SYNC),
            "dve": Eng("dve", nc.vector, mk("s_dve"), SAME_SYNC),
            "pool": Eng("pool", nc.gpsimd, mk("s_pool"), SAME_SYNC),
            "sp": Eng("sp", nc.sync, mk("s_sp"), False),
        }
        self.q = {
            "sp": DmaQ(self.eng["sp"], [mk("d_sp%d" % i) for i in range(12)]),
            "pool": DmaQ(self.eng["pool"], [mk("d_pl%d" % i) for i in range(8)]),
        }
        self.n_inst = 0

    def _waits(self, E, reads, writes, partial):
        need = {}

        def add(d):
            for s, v in d.items():
                if need.get(s, (None, 0))[1] < v:
                    need[s] = (s, v)

        for b in reads:
            add(b.w)
            if b.excl:
                add({s_: v_ for s_, v_ in b.r.items() if s_ is not E.sem})
        for b in writes:
            add(b.r)
            if not partial:
                add(b.w)
        for s, v in need.values():
            if s is E.sem and not E.same_sync:
                continue
            if E.known.get(id(s), 0) >= v:
                continue
            E.obj.wait_ge(s, v)
            E.known[id(s)] = v
            self.n_inst += 1

    def _mark(self, tok_sem, tok_val, reads, writes, partial):
        for b in reads:
            if b.r.get(tok_sem, 0) < tok_val:
                b.r[tok_sem] = tok_val
        for b in writes:
            if partial:
                b.w[tok_sem] = tok_val
            else:
                b.w = {tok_sem: tok_val}
                b.r = {}

    def op(self, eng, fn, reads=(), writes=(), partial=False):
        E = self.eng[eng]
        reads = _with_guards(reads, writes)
        self._waits(E, reads, writes, partial)
        inst = fn(E.obj)
        E.count += 1
        inst.then_inc(E.sem, 1)
        self.n_inst += 1
        self._mark(E.sem, E.count, reads, writes, partial)
        return inst

    def mm(self, mms, reads=(), writes=(), partial=False):
        E = self.eng["pe"]
        reads = _with_guards(reads, writes)
        self._waits(E, reads, writes, partial)
        inst = None
        for f in mms:
            inst = f(E.obj)
            self.n_inst += 1
        E.count += 1
        inst.then_inc(E.sem, 1)
        self._mark(E.sem, E.count, reads, writes, partial)

    def dma(self, qn, out, in_, reads=(), writes=(), partial=False, **kw):
        Q = self.q[qn]
        E = Q.eng
        reads = _with_guards(reads, writes)
        self._waits(E, reads, writes, partial)
        slot = Q.n % len(Q.sems)
        target = 16 * (Q.n // len(Q.sems) + 1)
        sem = Q.sems[slot]
        if target > 16 and E.known.get(id(sem), 0) < target - 16:
            E.obj.wait_ge(sem, target - 16)
            E.known[id(sem)] = target - 16
        Q.n += 1
        E.obj.dma_start(out=out, in_=in_, **kw).then_inc(sem, 16)
        self.n_inst += 1
        self._mark(sem, target, reads, writes, partial)

    def barrier(self, full=False):
        toks = [(E.sem, E.count) for E in self.eng.values() if E.count > 0]
        for Q in (self.q.values() if full else ()):
            nq = len(Q.sems)
            for i, s in enumerate(Q.sems):
                cnt = (Q.n - i + nq - 1) // nq if Q.n > i else 0
                if cnt > 0:
                    toks.append((s, 16 * cnt))
        for E in self.eng.values():
            if E.name == "sp" and not full:
                continue
            for s, v in toks:
                if s is E.sem:
                    continue
                if E.known.get(id(s), 0) >= v:
                    continue
                E.obj.wait_ge(s, v)
                E.known[id(s)] = v
                self.n_inst += 1


class Arena:
    def __init__(self, tensor, nbytes):
        self.t, self.n, self.off = tensor, nbytes, 0

    def reset(self):
        self.off = 0

    def alloc(self, shape_free, dtype, nparts=128):
        esz = {F32: 4, BF16: 2, I32: 4}[dtype]
        n = esz
        for s in shape_free:
            n *= s
        off = (self.off + 31) // 32 * 32
        assert off + n <= self.n, ("arena overflow", off, n, self.n)
        self.off = off + n
        ap = self.t[0:nparts, off:off + n].bitcast(dtype)
        if len(shape_free) == 2:
            ap = ap.rearrange("p (a b) -> p a b", a=shape_free[0])
        elif len(shape_free) == 3:
            ap = ap.rearrange("p (a b c) -> p a b c", a=shape_free[0], b=shape_free[1])
        elif len(shape_free) == 4:
            ap = ap.rearrange("p (a b c d) -> p a b c d", a=shape_free[0], b=shape_free[1], c=shape_free[2])
        return ap


def make_consts():
    c = {}
    c["ident"] = np.eye(128, dtype=np.float32)
    s = np.arange(128)[:, None]
    t = np.arange(128)[None, :]
    c["mrev"] = ((s > t) & (s // 64 == t // 64)).astype(np.float32)
    c["cind"] = np.stack([(np.arange(128) < 64), (np.arange(128) >= 64)], 1).astype(np.float32)
    c["mvals"] = np.tile(np.arange(-3, NM - 3, dtype=np.float32)[None, :], (128, 1))
    s4 = (np.arange(128) // 32)[:, None, None]
    m = np.arange(32)[None, :, None]
    c["tabmask"] = np.broadcast_to((m >= s4), (128, 32, 32)).astype(np.float32).reshape(128, 1024)
    order = ["ident", "mrev", "cind", "mvals", "tabmask"]
    offs, o = {}, 0
    for k in order:
        offs[k] = (o, c[k].shape[1])
        o += c[k].shape[1]
    return np.concatenate([c[k] for k in order], axis=1), offs


CONSTS_NP, CONST_OFFS = make_consts()


class _Stop(Exception):
    pass


def build_program(n_layers=DEPTH, final_norm=True, dumps=(), stop=None):
    nc = bass.Bass("TRN2", target_bir_lowering=False)
    es = contextlib.ExitStack()
    dt_in = lambda name, shape: nc.dram_tensor(name, list(shape), F32, kind="ExternalInput").ap()
    xT = dt_in("xT", (D, L))
    consts_d = dt_in("consts", CONSTS_NP.shape)
    P = {}
    for name, shape in [("attn_norm_g", (DEPTH, D)), ("w_in", (DEPTH, D, DIN)), ("s5_a_re", (DEPTH, 32, 64)),
                        ("s5_a_im", (DEPTH, 32, 64)), ("s5_log_dt", (DEPTH, 32)), ("s5_b_re", (DEPTH, 32, 64, 16)),
                        ("s5_b_im", (DEPTH, 32, 64, 16)), ("s5_c_re", (DEPTH, 32, 16, 64)),
                        ("s5_c_im", (DEPTH, 32, 16, 64)), ("s5_d", (DEPTH, 512)), ("s5_w_glu", (DEPTH, 512, 1024)),
                        ("gla_w_gate_up", (DEPTH, 16, 512)), ("gla_b_gate", (DEPTH, 512)),
                        ("gla_head_norm_g", (DEPTH, 1024)), ("w_branch_s5", (DEPTH, 512, 1024)),
                        ("w_branch_gla", (DEPTH, 1024, 1024)), ("w_out", (DEPTH, 1024, 1024)),
                        ("ffn_norm_g", (DEPTH, D)), ("w_ffn_gate", (DEPTH, D, DFF)), ("w_ffn_up", (DEPTH, D, DFF)),
                        ("w_ffn_down", (DEPTH, DFF, D)), ("final_norm_g", (D,))]:
        P[name] = dt_in(name, shape)
    outT = nc.dram_tensor("outT", [D, L], F32, kind="ExternalOutput").ap()
    hT = nc.dram_tensor("hT_scr", [D, L], F32, kind="Internal").ap()
    WB = {}
    for name in WEIGHT_NAMES:
        r, c = WSHAPES[name]
        WB[name] = nc.dram_tensor("wb_" + name, [n_layers, r, c], BF16, kind="Internal").ap()
    dump_aps = {}
    for name, shape in dumps:
        dump_aps[name] = nc.dram_tensor("dbg_" + name, list(shape), F32, kind="ExternalOutput").ap()

    es.enter_context(nc.Block())
    kb = KB(nc, es)
    op, mm, dma = kb.op, kb.mm, kb.dma
    es.enter_context(nc.allow_non_contiguous_dma(reason="small param loads"))

    sb = lambda name, shape, dt: nc.alloc_sbuf_tensor(name, list(shape), dt)
    cst = sb("cst", CONSTS_NP.shape, F32)
    cst_b = Buf("cst")

    def cview(k):
        o, n = CONST_OFFS[k]
        return cst[:, o:o + n]

    ident_f = cview("ident")
    ident_b = sb("ident_b", (128, 128), BF16)
    ones_b = sb("ones_b", (128, 128), BF16)
    mrev = cview("mrev")
    cind = cview("cind")
    mvals = cview("mvals")
    tabmask = cview("tabmask")
    g1col = sb("g1col", (128, DEPTH, KT), F32)
    g2col = sb("g2col", (128, DEPTH, KT), F32)
    hngcol = sb("hngcol", (128, DEPTH, KT), F32)
    gfcol = sb("gfcol", (128, KT), F32)
    small_b = Buf("small")
    yfm = sb("yfm", (128, 4, L), BF16)
    yfm_b = [Buf("yfm%d" % i) for i in range(4)]
    Sst = sb("Sst", (128, 4, 256), F32)
    Sst_b = [Buf("S%d" % i) for i in range(4)]
    alow = sb("alow", (32, TB), BF16)
    alow_b = Buf("alow")
    alowz_b = Buf("alowz")
    PS = [nc.alloc_psum_tensor("ps%d" % i, [128, 512], F32) for i in range(8)]
    PSb = [Buf("ps%d" % i, excl=True) for i in range(8)]

    ARENA_BYTES = nc.sbuf_bytes_remaining - 256
    arena_t = sb("arena", (128, ARENA_BYTES), U8)
    ar = Arena(arena_t, ARENA_BYTES)

    def dump(name, ap, rbufs):
        if name in dump_aps:
            dma("pool", dump_aps[name], ap, reads=rbufs)

    dma("sp", cst[:, :], consts_d[:, :], writes=[cst_b])
    op("dve", lambda e: e.tensor_copy(out=ident_b[:, :], in_=ident_f), reads=[cst_b], writes=[small_b])
    op("dve", lambda e: e.memset(ones_b[:, :], 1.0), writes=[small_b], partial=True)
    op("dve", lambda e: e.memset(alow[:, :], 1.0), writes=[alow_b, alowz_b])
    for l in range(DEPTH):
        dma("pool", g1col[:, l, :], P["attn_norm_g"][l].rearrange("(k p) -> p k", p=128), writes=[small_b], partial=True)
        dma("pool", g2col[:, l, :], P["ffn_norm_g"][l].rearrange("(k p) -> p k", p=128), writes=[small_b], partial=True)
        dma("pool", hngcol[:, l, :], P["gla_head_norm_g"][l].rearrange("(k p) -> p k", p=128), writes=[small_b], partial=True)
    dma("pool", gfcol[:, :], P["final_norm_g"].rearrange("(k p) -> p k", p=128), writes=[small_b], partial=True)
    wcast_bs = [Buf("wcast%d" % l) for l in range(n_layers)]
    wu_bs = [Buf("wu%d" % l) for l in range(n_layers)]
    WBu = nc.dram_tensor("wb_u", [n_layers, D, 512], BF16, kind="Internal").ap()

    def cast_wu(l):
        dma("pool", WBu[l], P["w_in"][l][:, O_U:O_U + 512], writes=[wu_bs[l]])

    def cast_pieces(l, step):
        out = []
        for name in WEIGHT_NAMES:
            r, c = WSHAPES[name]
            rows = r * c // 1024
            src = P[name][l].rearrange("r c -> (r c)").rearrange("(a b) -> a b", b=1024)
            dst = WB[name][l].rearrange("r c -> (r c)").rearrange("(a b) -> a b", b=1024)
            for r0 in range(0, rows, step):
                r1 = min(rows, r0 + step)
                out.append(lambda src=src, dst=dst, r0=r0, r1=r1: dma("pool", dst[r0:r1, :], src[r0:r1, :], writes=[wcast_bs[l]], partial=True))
        return out

    hd_b = [Buf("hT%d" % i) for i in range(NBLK)]
    cast_wu(0)
    for f in cast_pieces(0, 8192):
        f()
    cast_q = []

    NSLOT = 4
    SLOT_BYTES = 8192
    ring = {"t": None, "b": None, "n": 0, "wc": None}

    def wload(src, kt, ncols):
        i = ring["n"] % NSLOT
        ring["n"] += 1
        assert kt * ncols * 2 <= SLOT_BYTES
        ap = ring["t"][i][:, 0, 0:kt * ncols].rearrange("p (k c) -> p k c", k=kt)
        dma("sp", ap, src.rearrange("(k p) c -> p k c", p=128), reads=[ring["wc"]], writes=[ring["b"][i]])
        return ap, ring["b"][i]

    def rms_block(hsrc, hbuf, gcol_ap, sqt, sq_b, psi, rs_out, rs_b, xn, xn_b, tmp, tmp_b, perm=False):
        for k in range(KT):
            op("act", lambda e, k=k: e.activation(out=sqt[:, k, :], in_=hsrc[:, k, :], func=AF.Square),
               reads=[hbuf], writes=[sq_b], partial=(k > 0))
        mm([lambda e, k=k: e.matmul(PS[psi][:, :], lhsT=ones_b[:, :], rhs=sqt[:, k, :], start=(k == 0), stop=(k == KT - 1))
            for k in range(KT)], reads=[sq_b, small_b], writes=[PSb[psi]])
        op("act", lambda e: e.activation(out=tmp, in_=PS[psi][:, :], func=AF.Sqrt, scale=1.0 / D, bias=EPS),
           reads=[PSb[psi]], writes=[tmp_b])
        op("dve", lambda e: e.reciprocal(out=rs_out, in_=tmp), reads=[tmp_b], writes=[rs_b])
        if xn is not None:
            make_xn(hsrc, hbuf, gcol_ap, rs_out, rs_b, xn, xn_b, perm)

    def make_xn(hsrc, hbuf, gcol_ap, rs, rs_b, xn, xn_b, perm=False):
        nat = lambda a: a.rearrange("p (c q) -> p q c", q=32)
        prm = lambda a: a.rearrange("p (q c) -> p q c", q=32)
        for k in range(KT):
            if perm:
                o_, i0, i1 = prm(xn[:, k, :]), nat(hsrc[:, k, :]), nat(rs)
            else:
                o_, i0, i1 = xn[:, k, :], hsrc[:, k, :], rs
            op("dve", lambda e, k=k, o_=o_, i0=i0, i1=i1: e.scalar_tensor_tensor(out=o_, in0=i0, scalar=gcol_ap[:, k:k + 1],
                                                                               in1=i1, op0=ALU.mult, op1=ALU.mult),
               reads=[hbuf, rs_b, small_b], writes=[xn_b[k]])

    def chk(name):
        if stop == name:
            raise _Stop()

    try:
      chk("setup")
      for l in range(n_layers):
        hsrc_d = xT if l == 0 else hT
        last = (l == n_layers - 1)
        kb.barrier(full=True)
        ar.reset()
        CTre = ar.alloc((16, 16), F32)
        CTim = ar.alloc((16, 16), F32)
        Dcol = ar.alloc((1, 16), F32)
        PWre = ar.alloc((16, NM), F32)
        PWim = ar.alloc((16, NM), F32)
        nPWim = ar.alloc((16, NM), F32)
        Epre = ar.alloc((16, 4, 2, 16), BF16)
        Epim = ar.alloc((16, 4, 2, 16), BF16)
        CAB = ar.alloc((2, 2, 16), F32)
        p_mark = ar.off
        Ain = ar.alloc((3, 128), F32, nparts=16)
        ldt16 = ar.alloc((1, 2), F32, nparts=16)
        AT = ar.alloc((3, 16), F32)
        Bre = ar.alloc((16, 16), F32)
        Bim = ar.alloc((16, 16), F32)
        Cin = ar.alloc((2, 2, 128), F32)
        t40a = ar.alloc((16, NM), F32)
        t40b = ar.alloc((16, NM), F32)
        t40c = ar.alloc((16, NM), F32)
        t40i = ar.alloc((16, NM), I32)
        s16 = [ar.alloc((1, 16), F32) for _ in range(10)]
        gre = ar.alloc((16, 4), F32)
        gim = ar.alloc((16, 4), F32)
        g4a = ar.alloc((16, 4), F32)
        g4b = ar.alloc((16, 4), F32)
        e1 = ar.alloc((16, 16), F32)
        e2 = ar.alloc((16, 16), F32)
        prm_b = Buf("prm")
        pw_b = Buf("pw")
        ep_b = Buf("ep")

        dma("sp", Ain[:, 0, :], P["s5_a_re"][l].rearrange("(a g) p -> a (g p)", g=2), writes=[prm_b], partial=True)
        dma("sp", Ain[:, 1, :], P["s5_a_im"][l].rearrange("(a g) p -> a (g p)", g=2), writes=[prm_b], partial=True)
        dma("sp", ldt16[:, 0, :], P["s5_log_dt"][l].rearrange("(a g) -> a g", g=2), writes=[prm_b], partial=True)
        for g2 in range(2):
            dma("sp", Bre[64 * g2:64 * g2 + 64, :, :], P["s5_b_re"][l].rearrange("(a g) p c -> g p a c", g=2)[g2],
                writes=[prm_b], partial=True)
            dma("sp", Bim[64 * g2:64 * g2 + 64, :, :], P["s5_b_im"][l].rearrange("(a g) p c -> g p a c", g=2)[g2],
                writes=[prm_b], partial=True)
        for ri, nm in enumerate(("s5_c_re", "s5_c_im")):
            for half in range(2):
                for a8 in range(8):
                    pi = half * 8 + a8
                    dma("sp", Cin[16 * a8:16 * a8 + 16, ri, half, :].rearrange("c (g p) -> c g p", g=2),
                        P[nm][l][2 * pi:2 * pi + 2].rearrange("g c p -> c g p"), writes=[prm_b], partial=True)
        for s4 in range(4):
            dma("sp", Dcol[32 * s4:32 * s4 + 32, 0, :], P["s5_d"][l].rearrange("(a q) -> q a", q=32),
                writes=[prm_b], partial=True)
        op("dve", lambda e: e.tensor_copy(out=Ain[:, 2, :].rearrange("a (g p) -> a g p", g=2),
                                          in_=ldt16[:, 0, :].unsqueeze(2).to_broadcast([16, 2, 64])),
           reads=[prm_b], writes=[prm_b])
        mm([lambda e, i=i: e.transpose(PS[0][:, 16 * i:16 * i + 16], Ain[:, i, :], ident_f[0:16, 0:16]) for i in range(3)],
           reads=[prm_b, cst_b], writes=[PSb[0]])
        op("dve", lambda e: e.tensor_copy(out=AT[:, :, :], in_=PS[0][:, 0:48].rearrange("p (a b) -> p a b", a=3)),
           reads=[PSb[0]], writes=[prm_b], partial=True)
        for ri, CT in enumerate((CTre, CTim)):
            mm([lambda e, h=h, ri=ri: e.transpose(PS[1][:, 128 * h:128 * h + 128], Cin[:, ri, h, :], ident_f)
                for h in range(2)], reads=[prm_b, cst_b], writes=[PSb[1]])
            op("dve", lambda e, CT=CT: e.tensor_copy(out=CT[:, :, :], in_=PS[1][:, 0:256].rearrange("p (a c) -> p a c", c=16)),
               reads=[PSb[1]], writes=[prm_b], partial=True)
        are, aim, ldt = AT[:, 0, :], AT[:, 1, :], AT[:, 2, :]
        dtv, lam, om, den, rden, abr, cre, cim, tA, tB = [s[:, 0, :] for s in s16]
        pb = [prm_b]
        op("act", lambda e: e.activation(out=dtv, in_=ldt, func=AF.Exp), reads=pb, writes=pb)
        op("dve", lambda e: e.tensor_tensor(out=lam, in0=are, in1=dtv, op=ALU.mult), reads=pb, writes=pb)
        op("dve", lambda e: e.tensor_tensor(out=om, in0=aim, in1=dtv, op=ALU.mult), reads=pb, writes=pb)
        bc_m = lambda a: a.unsqueeze(2).to_broadcast([128, 16, NM])
        mv_bc = mvals.unsqueeze(1).to_broadcast([128, 16, NM])
        op("dve", lambda e: e.tensor_tensor(out=t40a[:, :, :], in0=bc_m(lam), in1=mv_bc, op=ALU.mult), reads=pb + [cst_b], writes=[pw_b])
        op("act", lambda e: e.activation(out=t40a[:, :, :], in_=t40a[:, :, :], func=AF.Exp), reads=[pw_b], writes=[pw_b])
        op("dve", lambda e: e.tensor_tensor(out=t40b[:, :, :], in0=bc_m(om), in1=mv_bc, op=ALU.mult), reads=pb + [cst_b], writes=[pw_b])
        op("dve", lambda e: e.tensor_scalar(out=t40b[:, :, :], in0=t40b[:, :, :], scalar1=1.0 / (2 * np.pi), scalar2=None, op0=ALU.mult),
           reads=[pw_b], writes=[pw_b])
        for (shift, dst, neg) in ((0.0, PWim, nPWim), (0.25, PWre, None)):
            src = t40b
            if shift != 0.0:
                op("dve", lambda e: e.tensor_scalar(out=t40c[:, :, :], in0=t40b[:, :, :], scalar1=shift, scalar2=None, op0=ALU.add),
                   reads=[pw_b], writes=[pw_b])
                src = t40c
            op("dve", lambda e, src=src: e.tensor_copy(out=t40i[:, :, :], in_=src[:, :, :]), reads=[pw_b], writes=[pw_b])
            op("dve", lambda e, dst=dst: e.tensor_copy(out=dst[:, :, :], in_=t40i[:, :, :]), reads=[pw_b], writes=[pw_b])
            op("dve", lambda e, src=src, dst=dst: e.tensor_tensor(out=dst[:, :, :], in0=src[:, :, :], in1=dst[:, :, :], op=ALU.subtract),
               reads=[pw_b], writes=[pw_b])
            op("act", lambda e, dst=dst: e.activation(out=dst[:, :, :], in_=dst[:, :, :], func=AF.Sin, scale=TWO_PI), reads=[pw_b], writes=[pw_b])
            op("dve", lambda e, dst=dst: e.tensor_tensor(out=dst[:, :, :], in0=dst[:, :, :], in1=t40a[:, :, :], op=ALU.mult),
               reads=[pw_b], writes=[pw_b])
        op("dve", lambda e: e.tensor_scalar(out=nPWim[:, :, :], in0=PWim[:, :, :], scalar1=-1.0, scalar2=None, op0=ALU.mult),
           reads=[pw_b], writes=[pw_b])
        MI = lambda m: m + 3
        rw = dict(reads=pb + [pw_b], writes=pb)
        op("dve", lambda e: e.tensor_scalar(out=abr, in0=PWre[:, :, MI(1)], scalar1=-1.0, scalar2=None, op0=ALU.add), **rw)
        abi = PWim[:, :, MI(1)]
        op("dve", lambda e: e.tensor_tensor(out=den, in0=are, in1=are, op=ALU.mult), **rw)
        op("dve", lambda e: e.tensor_tensor(out=tA, in0=aim, in1=aim, op=ALU.mult), **rw)
        op("dve", lambda e: e.tensor_tensor(out=den, in0=den, in1=tA, op=ALU.add), **rw)
        op("dve", lambda e: e.reciprocal(out=rden, in_=den), **rw)
        op("dve", lambda e: e.tensor_tensor(out=tA, in0=abr, in1=are, op=ALU.mult), **rw)
        op("dve", lambda e: e.tensor_tensor(out=tB, in0=abi, in1=aim, op=ALU.mult), **rw)
        op("dve", lambda e: e.tensor_tensor(out=tA, in0=tA, in1=tB, op=ALU.add), **rw)
        op("dve", lambda e: e.tensor_tensor(out=cre, in0=tA, in1=rden, op=ALU.mult), **rw)
        op("dve", lambda e: e.tensor_tensor(out=tA, in0=abi, in1=are, op=ALU.mult), **rw)
        op("dve", lambda e: e.tensor_tensor(out=tB, in0=abr, in1=aim, op=ALU.mult), **rw)
        op("dve", lambda e: e.tensor_tensor(out=tA, in0=tA, in1=tB, op=ALU.subtract), **rw)
        op("dve", lambda e: e.tensor_tensor(out=cim, in0=tA, in1=rden, op=ALU.mult), **rw)
        bc4 = lambda a: a.unsqueeze(2).to_broadcast([128, 16, 4])
        pre4, pim4 = PWre[:, :, MI(0):MI(4)], PWim[:, :, MI(0):MI(4)]
        op("dve", lambda e: e.tensor_tensor(out=g4a[:, :, :], in0=pre4, in1=bc4(cre), op=ALU.mult), **rw)
        op("dve", lambda e: e.tensor_tensor(out=g4b[:, :, :], in0=pim4, in1=bc4(cim), op=ALU.mult), **rw)
        op("dve", lambda e: e.tensor_tensor(out=gre[:, :, :], in0=g4a[:, :, :], in1=g4b[:, :, :], op=ALU.subtract), **rw)
        op("dve", lambda e: e.tensor_tensor(out=g4a[:, :, :], in0=pre4, in1=bc4(cim), op=ALU.mult), **rw)
        op("dve", lambda e: e.tensor_tensor(out=g4b[:, :, :], in0=pim4, in1=bc4(cre), op=ALU.mult), **rw)
        op("dve", lambda e: e.tensor_tensor(out=gim[:, :, :], in0=g4a[:, :, :], in1=g4b[:, :, :], op=ALU.add), **rw)
        op("pool", lambda e: e.memset(Epre[:, :, :, :, :], 0.0), writes=[ep_b])
        op("pool", lambda e: e.memset(Epim[:, :, :, :, :], 0.0), writes=[ep_b], partial=True)
        for s4 in range(4):
            m_ = 3 - s4
            bc16 = lambda a: a.unsqueeze(2).to_broadcast([128, 16, 16])
            gr, gi = gre[:, :, m_], gim[:, :, m_]
            rwe = dict(reads=pb + [ep_b], writes=[ep_b])
            op("dve", lambda e: e.tensor_tensor(out=e1[:, :, :], in0=Bre[:, :, :], in1=bc16(gr), op=ALU.mult), **rwe)
            op("dve", lambda e: e.tensor_tensor(out=e2[:, :, :], in0=Bim[:, :, :], in1=bc16(gi), op=ALU.mult), **rwe)
            for g2 in range(2):
                ps_ = slice(64 * g2, 64 * g2 + 64)
                op("dve", lambda e, ps_=ps_, g2=g2, s4=s4: e.tensor_tensor(out=Epre[ps_, :, s4, g2, :], in0=e1[ps_, :, :], in1=e2[ps_, :, :],
                                                                           op=ALU.subtract), **rwe)
            op("dve", lambda e: e.tensor_tensor(out=e1[:, :, :], in0=Bim[:, :, :], in1=bc16(gr), op=ALU.mult), **rwe)
            op("dve", lambda e: e.tensor_tensor(out=e2[:, :, :], in0=Bre[:, :, :], in1=bc16(gi), op=ALU.mult), **rwe)
            for g2 in range(2):
                ps_ = slice(64 * g2, 64 * g2 + 64)
                op("dve", lambda e, ps_=ps_, g2=g2, s4=s4: e.tensor_tensor(out=Epim[ps_, :, s4, g2, :], in0=e1[ps_, :, :], in1=e2[ps_, :, :],
                                                                           op=ALU.add), **rwe)
        op("dve", lambda e: e.tensor_copy(out=CAB[:, 0, 0, :], in_=PWre[:, :, MI(32)]), **rw)
        op("dve", lambda e: e.tensor_copy(out=CAB[:, 0, 1, :], in_=PWre[:, :, MI(32)]), **rw)
        op("dve", lambda e: e.tensor_copy(out=CAB[:, 1, 0, :], in_=nPWim[:, :, MI(32)]), **rw)
        op("dve", lambda e: e.tensor_copy(out=CAB[:, 1, 1, :], in_=PWim[:, :, MI(32)]), **rw)
        prm_all = [prm_b, pw_b, ep_b]
        chk("P")
        kb.barrier(full=True)
        ar.off = p_mark

        U = ar.alloc((16, 8, 128), BF16)
        U_b = [Buf("U%d" % i) for i in range(16)]
        m1_mark = ar.off
        xn_all = ar.alloc((KT, L), BF16)
        SBK = 256
        NSB = L // SBK
        xa_b = [Buf("xa%d" % i) for i in range(NSB)]
        Wu = ar.alloc((KT, 512), BF16)
        Wu_b = Buf("Wu")
        dma("sp", Wu[:, :, :], WBu[l].rearrange("(k p) c -> p k c", p=128), reads=[wu_bs[l]], writes=[Wu_b])
        h32m = [ar.alloc((KT, SBK), F32) for _ in range(2)]
        h32m_b = [Buf("h32m%d" % i) for i in range(2)]
        sqm = ar.alloc((KT, SBK), BF16)
        sqm_b = Buf("sqm")
        rsms = [ar.alloc((1, SBK), F32)[:, 0, :] for _ in range(2)]
        rsms_b = [Buf("rsm%d" % i) for i in range(2)]
        tmpm = ar.alloc((1, SBK), F32)[:, 0, :]
        tmpm_b = Buf("tmpm")
        uT = [ar.alloc((4, 512), BF16) for _ in range(2)]
        uT_b = [[Buf("uT%d_%d" % (i, q)) for q in range(4)] for i in range(2)]
        for k in range(KT):
            op("pool", lambda e, k=k: e.tensor_scalar(out=Wu[:, k, :], in0=Wu[:, k, :], scalar1=g1col[:, l, k:k + 1], scalar2=None, op0=ALU.mult),
               reads=[Wu_b, small_b], writes=[Wu_b])
        rstdT = ar.alloc((1, 32), F32)[:, 0, :]
        rstdT_b = Buf("rstdT")
        def m1_load(sbk):
            dma("sp", h32m[sbk % 2][:, :, :], hsrc_d[:, sbk * SBK:(sbk + 1) * SBK].rearrange("(k p) t -> p k t", p=128),
                reads=[hd_b[sbk // 2]], writes=[h32m_b[sbk % 2]])

        m1_load(0)
        for sbk in range(NSB):
            hm, hm_b = h32m[sbk % 2], h32m_b[sbk % 2]
            tsb = slice(sbk * SBK, (sbk + 1) * SBK)
            if sbk + 1 < NSB:
                m1_load(sbk + 1)
            op("act", lambda e: e.activation(out=sqm[:, :, :], in_=hm[:, :, :], func=AF.Square), reads=[hm_b], writes=[sqm_b])
            mm([lambda e, k=k: e.matmul(PS[4][:, 0:SBK], lhsT=ones_b[:, :], rhs=sqm[:, k, :], start=(k == 0), stop=(k == KT - 1)) for k in range(KT)],
               reads=[sqm_b, small_b], writes=[PSb[4]])
            op("act", lambda e: e.activation(out=tmpm, in_=PS[4][:, 0:SBK], func=AF.Sqrt, scale=1.0 / D, bias=EPS), reads=[PSb[4]], writes=[tmpm_b])
            rs_, rs_b_ = rsms[sbk % 2], rsms_b[sbk % 2]
            op("dve", lambda e, rs_=rs_: e.reciprocal(out=rs_, in_=tmpm), reads=[tmpm_b], writes=[rs_b_])
            dma("sp", rstdT[8 * sbk:8 * sbk + 8, :], rs_[0:1, :].rearrange("a (k q) -> a k q", q=32), reads=[rs_b_], writes=[rstdT_b], partial=True)
            if sbk % 2 == 0:
                op("dve", lambda e: e.tensor_copy(out=xn_all[:, :, tsb], in_=hm[:, :, :]), reads=[hm_b], writes=[xa_b[sbk]])
            else:
                op("act", lambda e: e.activation(out=xn_all[:, :, tsb], in_=hm[:, :, :], func=AF.Identity), reads=[hm_b], writes=[xa_b[sbk]])
        chk("M1a")
        for j in range(8):
            if j == 1:
                chk("M1b")
            ub, ub_b = uT[j % 2], uT_b[j % 2]
            for s4 in range(4):
                bank = s4 % 2
                pos = 4 * j + s4
                mm([lambda e, k=k, pos=pos, bank=bank: e.matmul(
                    PS[bank][:, :], lhsT=xn_all[:, k, :].rearrange("p (c q) -> p q c", q=32)[:, pos, :], rhs=Wu[:, k, :],
                    start=(k == 0), stop=(k == KT - 1)) for k in range(KT)],
                   reads=[Wu_b] + xa_b, writes=[PSb[bank]])
                if bank == 0:
                    op("act", lambda e, s4=s4, pos=pos: e.activation(out=ub[:, s4, :], in_=PS[0][:, :], func=AF.Identity, scale=rstdT[:, pos:pos + 1]),
                       reads=[PSb[0], rstdT_b], writes=[ub_b[s4]])
                else:
                    op("dve", lambda e, s4=s4, pos=pos: e.tensor_scalar(out=ub[:, s4, :], in0=PS[1][:, :], scalar1=rstdT[:, pos:pos + 1], scalar2=None, op0=ALU.mult),
                       reads=[PSb[1], rstdT_b], writes=[ub_b[s4]])
            if j == 0:
                chk("M1c")
            for g in range(4):
                tb = 2 + (g % 2)
                psh = PS[tb][:, 0:256].bitcast(BF16)
                mm([lambda e, pq=pq, s4=s4, g=g, psh=psh: e.transpose(psh[32 * s4:32 * s4 + 32, 128 * pq:128 * pq + 128],
                                                                     ub[:, s4, 32 * (4 * g + pq):32 * (4 * g + pq) + 32], ident_b[:, :],
                                                                     tile_position=(0, 32 * s4))
                    for pq in range(4) for s4 in range(4)],
                   reads=ub_b + [small_b], writes=[PSb[tb]])
                src = psh.rearrange("p (a k) -> p a k", a=4)
                dst = U[:, 4 * g:4 * g + 4, j, :]
                if g % 2 == 0:
                    op("act", lambda e, src=src, dst=dst: e.activation(out=dst, in_=src, func=AF.Identity), reads=[PSb[tb]], writes=U_b[4 * g:4 * g + 4], partial=True)
                else:
                    op("dve", lambda e, src=src, dst=dst: e.tensor_copy(out=dst, in_=src), reads=[PSb[tb]], writes=U_b[4 * g:4 * g + 4], partial=True)
        if "U" in dump_aps:
            for pi_ in range(16):
                dma("pool", dump_aps["U"][:, 1024 * pi_:1024 * pi_ + 1024], U[:, pi_, :, :].rearrange("p j k -> p (j k)"), reads=U_b)

        chk("M1")
        ar.off = m1_mark
        kb.barrier(full=True)
        H = ar.alloc((129, 3, 16), F32)
        H_b = Buf("H")
        Zs_b = H_b
        Hb = ar.alloc((2, 16, 128), BF16)
        Hb_b = Buf("Hb")
        TA = ar.alloc((2, 16), F32)
        TBt = ar.alloc((2, 16), F32)
        S1 = ar.alloc((2, 16), F32)
        sc_b = Buf("scan_tmp")
        NPB = 2
        Dg = [[ar.alloc((8, 128), BF16) for _ in range(3)] for _ in range(NPB)]
        Dg_b = [Buf("Dg%d" % i) for i in range(NPB)]
        Zw = [ar.alloc((2, 8, 128), BF16) for _ in range(NPB)]
        Zw_b = [Buf("Zw%d" % i) for i in range(NPB)]
        for pi in range(16):
            pb_ = pi % NPB
            for ci, tab in enumerate((PWre, PWim, nPWim)):
                op("pool", lambda e, ci=ci, tab=tab, pb_=pb_, pi=pi: e.tensor_tensor(
                    out=Dg[pb_][ci][:, :, :], in0=ident_f.unsqueeze(1).to_broadcast([128, 8, 128]),
                    in1=tab[:, pi, MI(0):MI(32):4].unsqueeze(2).to_broadcast([128, 8, 128]), op=ALU.mult),
                   reads=[cst_b, pw_b], writes=[Dg_b[pb_]], partial=(ci > 0))
            Dre_, Dim_, Dnim_ = Dg[pb_]
            lre = Epre[:, pi, :, :, :].rearrange("p s g c -> p (s g c)")
            lim = Epim[:, pi, :, :, :].rearrange("p s g c -> p (s g c)")
            for comp in range(2):
                r1, r2 = (Dre_, Dnim_) if comp == 0 else (Dim_, Dre_)
                for hlf in range(2):
                    bank = 2 * comp + hlf
                    js = slice(4 * hlf, 4 * hlf + 4)
                    mm([lambda e, r1=r1, js=js, bank=bank: e.matmul(PS[bank][:, :], lhsT=lre, rhs=r1[:, js, :].rearrange("p j c -> p (j c)"),
                                                                  start=True, stop=False),
                        lambda e, r2=r2, js=js, bank=bank: e.matmul(PS[bank][:, :], lhsT=lim, rhs=r2[:, js, :].rearrange("p j c -> p (j c)"),
                                                                  start=False, stop=True)],
                       reads=[ep_b, Dg_b[pb_]], writes=[PSb[bank]])
                    eng = "act" if hlf == 0 else "dve"
                    dst = Zw[pb_][:, comp, js, :].rearrange("p j c -> p (j c)")
                    if eng == "act":
                        op("act", lambda e, dst=dst, bank=bank: e.activation(out=dst, in_=PS[bank][:, :], func=AF.Identity),
                           reads=[PSb[bank]], writes=[Zw_b[pb_]], partial=(comp + hlf > 0))
                    else:
                        op("dve", lambda e, dst=dst, bank=bank: e.tensor_copy(out=dst, in_=PS[bank][:, :]),
                           reads=[PSb[bank]], writes=[Zw_b[pb_]], partial=True)
            zb = 4 + (pi % 2)
            for comp in range(2):
                mm([lambda e, comp=comp, j=j, pb_=pb_, pi=pi, zb=zb: e.matmul(
                    PS[zb][:, 128 * comp:128 * comp + 128], lhsT=Zw[pb_][:, comp, 7 - j, :], rhs=U[:, pi, j, :],
                    start=(j == 0), stop=(j == 7)) for j in range(8)],
                   reads=[Zw_b[pb_], U_b[pi]], writes=[PSb[zb]], partial=(comp == 1))
            op("dve" if pi % 2 else "act",
               (lambda e, zb=zb, pi=pi: e.tensor_copy(out=H[:, 1:129, 0:2, pi].rearrange("p k c -> p c k"),
                                                    in_=PS[zb][:, 0:256].rearrange("p (c k) -> p c k", c=2))) if pi % 2 else
               (lambda e, zb=zb, pi=pi: e.activation(out=H[:, 1:129, 0:2, pi].rearrange("p k c -> p c k"),
                                                   in_=PS[zb][:, 0:256].rearrange("p (c k) -> p c k", c=2), func=AF.Identity)),
               reads=[PSb[zb]], writes=[Zs_b], partial=True)
        chk("S5A")
        op("dve", lambda e: e.memset(H[:, 0, :, :], 0.0), writes=[H_b], partial=True)
        hz = [H_b, sc_b] + prm_all
        for k in range(128):
            op("dve", lambda e, k=k: e.tensor_tensor(out=TA[:, :, :], in0=CAB[:, 0, :, :], in1=H[:, k, 0:2, :], op=ALU.mult), reads=hz, writes=[sc_b])
            op("dve", lambda e, k=k: e.tensor_tensor(out=TBt[:, :, :], in0=CAB[:, 1, :, :], in1=H[:, k, 1:3, :], op=ALU.mult), reads=hz, writes=[sc_b])
            op("dve", lambda e, k=k: e.tensor_tensor(out=S1[:, :, :], in0=H[:, k + 1, 0:2, :], in1=TA[:, :, :], op=ALU.add), reads=hz, writes=[sc_b])
            op("dve", lambda e, k=k: e.tensor_tensor(out=H[:, k + 1, 0:2, :], in0=S1[:, :, :], in1=TBt[:, :, :], op=ALU.add), reads=hz, writes=[H_b])
            op("dve", lambda e, k=k: e.tensor_copy(out=H[:, k + 1, 2, :], in_=H[:, k + 1, 0, :]), reads=hz, writes=[H_b])
        for comp in range(2):
            op("act", lambda e, comp=comp: e.activation(out=Hb[:, comp, :, :], in_=H[:, 0:128, comp, :].rearrange("p k a -> p a k"), func=AF.Identity),
               reads=[H_b], writes=[Hb_b], partial=(comp == 1))
        chk("S5S")
        FRE = [ar.alloc((36, 2, 16), BF16) for _ in range(NPB)]
        FIM = [ar.alloc((36, 2, 16), BF16) for _ in range(NPB)]
        F_b = [Buf("F%d" % i) for i in range(NPB)]
        TAB = [ar.alloc((32, 32), BF16) for _ in range(NPB)]
        TAB_b = [Buf("TAB%d" % i) for i in range(NPB)]
        Dd = [ar.alloc((1, 128), BF16) for _ in range(NPB)]
        Dd_b = [Buf("Dd%d" % i) for i in range(NPB)]
        fts = [[ar.alloc((36, 16), F32) for _ in range(4)] for _ in range(2)]
        ft_bs = [Buf("ft0"), Buf("ft1")]
        Fz_b = [Buf("Fz%d" % i) for i in range(NPB)]
        for i in range(NPB):
            op("pool", lambda e, i=i: e.memset(FRE[i][:, :, :, :], 0.0), writes=[F_b[i], Fz_b[i]])
            op("pool", lambda e, i=i: e.memset(FIM[i][:, :, :, :], 0.0), writes=[F_b[i], Fz_b[i]], partial=True)
        for pi in range(16):
            pb_ = pi % NPB
            bcn = lambda a: a.unsqueeze(1).to_broadcast([128, 36, 16])
            bcc = lambda a: a.unsqueeze(2).to_broadcast([128, 36, 16])
            ctr, cti = CTre[:, pi, :], CTim[:, pi, :]
            pwr, pwi, npwi = PWre[:, pi, 0:36], PWim[:, pi, 0:36], nPWim[:, pi, 0:36]
            fe = "pool" if pi % 2 == 0 else "dve"
            ft = fts[pi % 2]
            ft_b = ft_bs[pi % 2]
            rwf = dict(reads=prm_all + [ft_b], writes=[ft_b])
            op(fe, lambda e: e.tensor_tensor(out=ft[0][:, :, :], in0=bcn(ctr), in1=bcc(pwr), op=ALU.mult), **rwf)
            op(fe, lambda e: e.tensor_tensor(out=ft[1][:, :, :], in0=bcn(cti), in1=bcc(pwi), op=ALU.mult), **rwf)
            op(fe, lambda e: e.tensor_tensor(out=ft[2][:, :, :], in0=bcn(ctr), in1=bcc(npwi), op=ALU.mult), **rwf)
            op(fe, lambda e: e.tensor_tensor(out=ft[3][:, :, :], in0=bcn(cti), in1=bcc(pwr), op=ALU.mult), **rwf)
            for g2 in range(2):
                ps_ = slice(64 * g2, 64 * g2 + 64)
                op(fe, lambda e, ps_=ps_, g2=g2, pb_=pb_: e.tensor_tensor(out=FRE[pb_][ps_, :, g2, :], in0=ft[0][ps_, :, :], in1=ft[1][ps_, :, :],
                                                                        op=ALU.subtract), reads=[ft_b, Fz_b[pb_]], writes=[F_b[pb_]], partial=True)
                op(fe, lambda e, ps_=ps_, g2=g2, pb_=pb_: e.tensor_tensor(out=FIM[pb_][ps_, :, g2, :], in0=ft[2][ps_, :, :], in1=ft[3][ps_, :, :],
                                                                        op=ALU.subtract), reads=[ft_b, Fz_b[pb_]], writes=[F_b[pb_]], partial=True)
            op("pool", lambda e, pb_=pb_, pi=pi: e.tensor_scalar(out=Dd[pb_][:, 0, :], in0=ident_f, scalar1=Dcol[:, 0, pi:pi + 1], scalar2=None, op0=ALU.mult),
               reads=[cst_b, prm_b], writes=[Dd_b[pb_]])
            lre = Epre[:, pi, :, :, :].rearrange("p s g c -> p (s g c)")
            lim = Epim[:, pi, :, :, :].rearrange("p s g c -> p (s g c)")
            for hlf in range(2):
                bank = hlf
                ns = slice(16 * hlf, 16 * hlf + 16)
                ms = [lambda e, ns=ns, bank=bank, pb_=pb_: e.matmul(PS[bank][:, :], lhsT=lre, rhs=FRE[pb_][:, ns, :, :].rearrange("p n g c -> p (n g c)"),
                                                                  start=True, stop=False),
                      lambda e, ns=ns, bank=bank, pb_=pb_, hlf=hlf: e.matmul(PS[bank][:, :], lhsT=lim, rhs=FIM[pb_][:, ns, :, :].rearrange("p n g c -> p (n g c)"),
                                                                           start=False, stop=(hlf == 1))]
                if hlf == 0:
                    ms.append(lambda e, pb_=pb_: e.matmul(PS[0][:, 0:128], lhsT=ident_b[:, :], rhs=Dd[pb_][:, 0, :], start=False, stop=True))
                mm(ms, reads=[ep_b, F_b[pb_], Dd_b[pb_], small_b], writes=[PSb[bank]])
                op("dve", lambda e, bank=bank, pb_=pb_, hlf=hlf: e.tensor_tensor(
                    out=TAB[pb_][:, 16 * hlf:16 * hlf + 16, :].rearrange("p m c -> p (m c)"), in0=PS[bank][:, :],
                    in1=tabmask[:, 512 * hlf:512 * hlf + 512], op=ALU.mult),
                   reads=[PSb[bank], cst_b], writes=[TAB_b[pb_]], partial=(hlf == 1))
            for ib in range(2):
                bank = 2 + 2 * (pi % 2) + ib
                grp = []
                for i4 in range(4):
                    i = 4 * ib + i4
                    o_ = PS[bank][:, 128 * i4:128 * i4 + 128]
                    for j in range(i + 1):
                        dlt = 4 * (i - j)
                        grp.append(lambda e, o_=o_, dlt=dlt, j=j, pb_=pb_, pi=pi: e.matmul(
                            o_, lhsT=TAB[pb_][:, dlt:dlt + 4, :].rearrange("p m c -> p (m c)"), rhs=U[:, pi, j, :], start=(j == 0), stop=False))
                    grp.append(lambda e, o_=o_, i=i, pb_=pb_, pi=pi: e.matmul(
                        o_, lhsT=FRE[pb_][:, 4 * i + 4:4 * i + 8, :, :].rearrange("p n g c -> p (n g c)"), rhs=Hb[:, 0, pi, :], start=False, stop=False))
                    grp.append(lambda e, o_=o_, i=i, pb_=pb_, pi=pi: e.matmul(
                        o_, lhsT=FIM[pb_][:, 4 * i + 4:4 * i + 8, :, :].rearrange("p n g c -> p (n g c)"), rhs=Hb[:, 1, pi, :], start=False, stop=True))
                mm(grp, reads=[TAB_b[pb_], F_b[pb_], U_b[pi], Hb_b], writes=[PSb[bank]])
                for t4 in range(4):
                    src = PS[bank][32 * t4:32 * t4 + 32, :].rearrange("p (i k) -> p i k", i=4)
                    q = pi % 4
                    dst = yfm[32 * q:32 * q + 32, pi // 4, :].rearrange("p (i t k) -> p i t k", i=8, t=4)[:, 4 * ib:4 * ib + 4, t4, :]
                    op("dve", lambda e, src=src, dst=dst: e.tensor_copy(out=dst, in_=src),
                       reads=[PSb[bank]], writes=[yfm_b[pi // 4]], partial=True)
        if "yfm" in dump_aps:
            for ct in range(4):
                dma("pool", dump_aps["yfm"][128 * ct:128 * ct + 128, :], yfm[:, ct, :], reads=[yfm_b[ct]])

        chk("S5")
        kb.barrier(full=True)
        ar.reset()
        ring["t"] = [ar.alloc((1, SLOT_BYTES // 2), BF16) for _ in range(NSLOT)]
        ring["b"] = [Buf("ring%d" % i) for i in range(NSLOT)]
        ring["wc"] = wcast_bs[l]
        h32s = [ar.alloc((KT, TB), F32) for _ in range(2)]
        h32s_b = [Buf("h32_%d" % i) for i in range(2)]
        xns = [ar.alloc((KT, TB), BF16) for _ in range(2)]
        xns_b = [[Buf("xn%d_%d" % (i, k)) for k in range(KT)] for i in range(2)]
        ygla = ar.alloc((KT, TB), BF16)
        ygla_b = Buf("ygla")
        sqt, sq_b = ygla, ygla_b
        rsN = ar.alloc((1, TB), F32)[:, 0, :]
        rsN_b = Buf("rsN")
        tmpN = ar.alloc((1, TB), F32)[:, 0, :]
        tmpN_b = Buf("tmpN")
        sq2 = ar.alloc((2, TB), BF16)
        sq2_b = Buf("sq2")
        ygl = ar.alloc((4, TB), BF16)
        ygl_b = Buf("ygl")
        ys5 = ar.alloc((4, TB), BF16)
        ys5_b = Buf("ys5")
        wupa = ar.alloc((1, 512), BF16, nparts=32)
        wupa_b = Buf("wupa")
        wup32 = ar.alloc((1, 512), F32, nparts=32)
        rs2 = ar.alloc((1, TB), F32)[:, 0, :]
        rs2_b = Buf("rs2")
        tmpA = ar.alloc((1, TB), F32)[:, 0, :]
        tmpA_b = Buf("tmpA")
        tmpB = ar.alloc((1, TB), F32)[:, 0, :]
        tmpB_b = Buf("tmpB")
        tbf = [ar.alloc((1, TB), BF16)[:, 0, :] for _ in range(2)]
        tbf_b = [Buf("tbf%d" % i) for i in range(2)]
        dec = ar.alloc((1, 32), F32)[:, 0, :]
        dec_b = Buf("dec")
        S_bf = ar.alloc((8, 256), BF16)
        Sbf_b = [Buf("Sbf%d" % i) for i in range(8)]
        PB = Buf("phase")
        gb = lambda n: Buf(n, guard=PB)
        r0 = ar.off
        sg5 = ar.alloc((KT, TB), BF16)
        sgg = ar.alloc((KT, TB), BF16)
        mixed = ar.alloc((KT, TB), BF16)
        hid = ar.alloc((FT, TB), BF16)
        r1 = ar.off
        ar.off = r0
        q_fm = ar.alloc((4, TB), BF16)
        kendT = ar.alloc((4, 512), BF16)
        vT = ar.alloc((4, 1024), BF16)
        Eexp = ar.alloc((4, 512), F32)
        o_sb = ar.alloc((8, TB), F32)
        sp32 = [ar.alloc((1, 512), F32)[:, 0, :] for _ in range(2)]
        e32 = ar.alloc((1, 512), F32)[:, 0, :]
        assert ar.off <= r1, (ar.off, r1)
        ar.off = r1
        sg5_b, sgg_b, mixed_b, hid_b = gb("sg5"), gb("sgg"), gb("mixed"), gb("hid")
        q_b = gb("q")
        sp32_b = [gb("sp%d" % i) for i in range(2)]
        e32_b = gb("e32")
        Eexp_b = [gb("Eexp%d" % i) for i in range(4)]
        kend_b = [gb("kend%d" % i) for i in range(4)]
        vT_b = [gb("vT%d" % i) for i in range(4)]
        o_b = gb("o")

        wupz_b = Buf("wupz")
        op("dve", lambda e: e.memset(wup32[:, 0, :], 0.0), writes=[wupz_b, wupa_b])
        dma("sp", wup32[0:16, 0, :], P["gla_w_gate_up"][l], writes=[wupa_b], partial=True, reads=[wupz_b])
        dma("sp", wup32[16:17, 0, :], P["gla_b_gate"][l].rearrange("(a c) -> a c", a=1), writes=[wupa_b], partial=True, reads=[wupz_b])
        op("dve", lambda e: e.tensor_copy(out=wupa[:, 0, :], in_=wup32[:, 0, :]), reads=[wupa_b], writes=[wupa_b])
        for h in range(4):
            op("dve", lambda e, h=h: e.memset(Sst[:, h, :], 0.0), writes=[Sst_b[h]])

        Wl = {n: WB[n][l] for n in WEIGHT_NAMES}

        def fm_tiles(wsrc, kt_n, col0, ncols, rhs_fn, consume, chunk=512, rbufs=()):
            m = 0
            for c0 in range(0, ncols, chunk):
                cw = min(chunk, ncols - c0)
                wap, wb = wload(wsrc[:, col0 + c0:col0 + c0 + cw], kt_n, cw)
                for t in range(0, cw, 128):
                    tw = min(128, cw - t)
                    bank = m % 2
                    mm([lambda e, k=k, t=t, tw=tw, bank=bank: e.matmul(PS[bank][0:tw, :], lhsT=wap[:, k, t:t + tw], rhs=rhs_fn(k),
                                                                      start=(k == 0), stop=(k == kt_n - 1)) for k in range(kt_n)],
                       reads=[wb] + list(rbufs), writes=[PSb[bank]])
                    consume(m, bank)
                    m += 1

        if not last:
            cast_q = [lambda: cast_wu(l + 1)] + cast_pieces(l + 1, 512)
        n_per = (len(cast_q) + 4 * NBLK - 1) // (4 * NBLK)

        def cast_some(n):
            for _ in range(n):
                if cast_q:
                    cast_q.pop(0)()

        for b in range(NBLK):
            tsl = slice(b * TB, (b + 1) * TB)
            h32, h32_b = h32s[b % 2], h32s_b[b % 2]
            xn, xn_b = xns[b % 2], xns_b[b % 2]
            if b == 0:
                dma("pool", h32[:, :, :], hsrc_d[:, tsl].rearrange("(k p) t -> p k t", p=128), reads=[hd_b[b]], writes=[h32_b])

            if b == 0:
                rms_block(h32, h32_b, g1col[:, l, :], sqt, sq_b, 6, rsN, rsN_b, xn, xn_b, tmpN, tmpN_b)
            xn_rhs = lambda k: xn[:, k, :]
            fm_tiles(Wl["w_in"], KT, O_Q, 512, xn_rhs,
                     lambda m, bank: op("act", lambda e: e.activation(out=q_fm[:, m, :], in_=PS[bank][:, :], func=AF.Identity, scale=128 ** -0.5),
                                        reads=[PSb[bank]], writes=[q_b] + ([PB] if m == 0 else []), partial=True), rbufs=xn_b)
            fm_tiles(Wl["w_in"], KT, O_A, 16, xn_rhs,
                     lambda m, bank: op("act", lambda e: e.activation(out=alow[0:16, :], in_=PS[bank][0:16, :], func=AF.Identity),
                                        reads=[PSb[bank], alowz_b], writes=[alow_b], partial=True), rbufs=xn_b)
            for tt in range(4):
                tsl128 = slice(128 * tt, 128 * tt + 128)
                sp_, spb = sp32[tt % 2], sp32_b[tt % 2]
                mm([lambda e: e.matmul(PS[2][:, :], lhsT=alow[0:17, tsl128], rhs=wupa[0:17, 0, :], start=True, stop=True)],
                   reads=[alow_b, wupa_b], writes=[PSb[2]])
                op("act", lambda e: e.activation(out=e32, in_=PS[2][:, :], func=AF.Exp, scale=-1.0), reads=[PSb[2]], writes=[e32_b])
                op("act", lambda e, sp_=sp_: e.activation(out=sp_, in_=e32, func=AF.Ln, bias=1.0), reads=[e32_b], writes=[spb])
                mm([lambda e, sp_=sp_: e.matmul(PS[3][:, :], lhsT=mrev, rhs=sp_, start=True, stop=True)], reads=[spb, cst_b], writes=[PSb[3]])
                op("act", lambda e, tt=tt: e.activation(out=Eexp[:, tt, :], in_=PS[3][:, :], func=AF.Exp, scale=-1.0 / 16.0),
                   reads=[PSb[3]], writes=[Eexp_b[tt]])
                mm([lambda e, h=h, sp_=sp_, tt=tt: e.matmul(PS[7][:, 8 * h + 2 * tt:8 * h + 2 * tt + 2], lhsT=sp_[:, 128 * h:128 * h + 128], rhs=cind,
                                                           start=True, stop=True) for h in range(4)],
                   reads=[spb, cst_b], writes=[PSb[7]], partial=(tt > 0))
            op("act", lambda e: e.activation(out=dec, in_=PS[7][:, 0:32], func=AF.Exp, scale=-1.0 / 16.0), reads=[PSb[7]], writes=[dec_b])
            wk, wk_b = wload(Wl["w_in"][:, O_K:O_K + 512], KT, 512)
            for tt in range(4):
                bank = tt % 2
                mm([lambda e, k=k, tt=tt, bank=bank: e.matmul(PS[bank][:, :], lhsT=xn[:, k, 128 * tt:128 * tt + 128], rhs=wk[:, k, :],
                                                            start=(k == 0), stop=(k == KT - 1)) for k in range(KT)],
                   reads=[wk_b] + xn_b, writes=[PSb[bank]])
                op("dve", lambda e, tt=tt, bank=bank: e.tensor_tensor(out=kendT[:, tt, :], in0=PS[bank][:, :], in1=Eexp[:, tt, :], op=ALU.mult),
                   reads=[PSb[bank], Eexp_b[tt]], writes=[kend_b[tt]])
            for k in range(4):
                op("act", lambda e, k=k: e.activation(out=ygl[:, k, :].rearrange("p (c q) -> p c q", q=32),
                                                    in_=yfm[:, k, :].rearrange("p (q c) -> p c q", q=32)[:, 16 * b:16 * b + 16, :],
                                                    func=AF.Gelu_apprx_tanh), reads=yfm_b, writes=[ygl_b], partial=(k > 0))
            for vc in range(2):
                wv, wv_b = wload(Wl["w_in"][:, O_V + 512 * vc:O_V + 512 * vc + 512], KT, 512)
                for tt in range(4):
                    bank = tt % 2
                    mm([lambda e, k=k, tt=tt, bank=bank: e.matmul(PS[bank][:, :], lhsT=xn[:, k, 128 * tt:128 * tt + 128], rhs=wv[:, k, :],
                                                                start=(k == 0), stop=(k == KT - 1)) for k in range(KT)],
                       reads=[wv_b] + xn_b, writes=[PSb[bank]])
                    op("dve", lambda e, tt=tt, bank=bank, vc=vc: e.tensor_copy(out=vT[:, tt, 512 * vc:512 * vc + 512], in_=PS[bank][:, :]),
                       reads=[PSb[bank]], writes=[vT_b[tt]], partial=(vc == 1))
            def gla_o(c):
                obank = 4 + (c % 2)
                sb_ = (c % 2) * 4
                mm([lambda e, h=h, dv=dv, obank=obank, c=c, sb_=sb_: e.matmul(PS[obank][:, 64 * (2 * h + dv):64 * (2 * h + dv) + 64],
                                                                          lhsT=S_bf[:, sb_ + h, 128 * dv:128 * dv + 128], rhs=q_fm[:, h, 64 * c:64 * c + 64],
                                                                          start=True, stop=True) for h in range(4) for dv in range(2)],
                   reads=Sbf_b[sb_:sb_ + 4] + [q_b], writes=[PSb[obank]])
                op("act", lambda e, obank=obank, c=c: e.activation(out=o_sb[:, :, 64 * c:64 * c + 64], in_=PS[obank][:, :].rearrange("p (a t) -> p a t", a=8),
                                                                 func=AF.Identity), reads=[PSb[obank]], writes=[o_b], partial=(c > 0))

            for c in range(8):
                tt, hf = c // 2, c % 2
                prt = slice(64 * hf, 64 * hf + 64)
                sb_ = (c % 2) * 4
                for hp in range(2):
                    bank = 2 + hp
                    mm([lambda e, h=h, bank=bank: e.matmul(PS[bank][:, 256 * (h % 2):256 * (h % 2) + 256], lhsT=kendT[prt, tt, 128 * h:128 * h + 128],
                                                         rhs=vT[prt, tt, 256 * h:256 * h + 256], start=True, stop=True) for h in (2 * hp, 2 * hp + 1)],
                       reads=[kend_b[tt], vT_b[tt]], writes=[PSb[bank]])
                    for h in (2 * hp, 2 * hp + 1):
                        op("dve", lambda e, h=h, bank=bank, c=c: e.scalar_tensor_tensor(
                            out=Sst[:, h, :], in0=Sst[:, h, :], scalar=dec[:, 8 * h + c:8 * h + c + 1],
                            in1=PS[bank][:, 256 * (h % 2):256 * (h % 2) + 256], op0=ALU.mult, op1=ALU.add),
                           reads=[PSb[bank], dec_b, Sst_b[h]], writes=[Sst_b[h]])
                        op("act", lambda e, h=h, sb_=sb_: e.activation(out=S_bf[:, sb_ + h, :], in_=Sst[:, h, :], func=AF.Identity),
                           reads=[Sst_b[h]], writes=[Sbf_b[sb_ + h]])
                if c >= 1:
                    gla_o(c - 1)
            gla_o(7)
            for h in range(4):
                op("act", lambda e, h=h: e.activation(out=sq2[:, :, :], in_=o_sb[:, 2 * h:2 * h + 2, :], func=AF.Square), reads=[o_b], writes=[sq2_b])
                mm([lambda e, dv=dv: e.matmul(PS[6][:, :], lhsT=ones_b[:, :], rhs=sq2[:, dv, :], start=(dv == 0), stop=(dv == 1)) for dv in range(2)],
                   reads=[sq2_b, small_b], writes=[PSb[6]])
                op("act", lambda e: e.activation(out=tmpA, in_=PS[6][:, :], func=AF.Sqrt, scale=1.0 / 256, bias=EPS), reads=[PSb[6]], writes=[tmpA_b])
                op("dve", lambda e: e.reciprocal(out=rs2, in_=tmpA), reads=[tmpA_b], writes=[rs2_b])

                def g_consume(m, bank, h=h):
                    i = 2 * h + m
                    tb, tbb = tbf[m % 2], tbf_b[m % 2]
                    op("act", lambda e: e.activation(out=tb, in_=PS[bank][:, :], func=AF.Silu), reads=[PSb[bank]], writes=[tbb])
                    op("dve", lambda e: e.scalar_tensor_tensor(out=tmpB, in0=o_sb[:, i, :], scalar=hngcol[:, l, i:i + 1], in1=rs2,
                                                               op0=ALU.mult, op1=ALU.mult), reads=[o_b, rs2_b, small_b], writes=[tmpB_b])
                    op("dve", lambda e: e.tensor_tensor(out=ygla[:, i, :], in0=tmpB, in1=tb, op=ALU.mult), reads=[tmpB_b, tbb], writes=[ygla_b],
                       partial=(i > 0))
                fm_tiles(Wl["w_in"], KT, O_G + 256 * h, 256, xn_rhs, g_consume, chunk=256, rbufs=xn_b)
            if b + 1 < NBLK:
                dma("pool", h32s[(b + 1) % 2][:, :, :], hsrc_d[:, (b + 1) * TB:(b + 2) * TB].rearrange("(k p) t -> p k t", p=128),
                    reads=[hd_b[b + 1]], writes=[h32s_b[(b + 1) % 2]])
            cast_some(n_per)
            yrhs = lambda k: ygl[:, k, :]
            wg0, wg0_b = wload(Wl["s5_w_glu"][:, 0:512], 4, 512)
            wg1, wg1_b = wload(Wl["s5_w_glu"][:, 512:1024], 4, 512)
            for m in range(4):
                mm([lambda e, k=k, m=m: e.matmul(PS[0][:, :], lhsT=wg0[:, k, 128 * m:128 * m + 128], rhs=yrhs(k), start=(k == 0), stop=(k == 3)) for k in range(4)],
                   reads=[wg0_b, ygl_b], writes=[PSb[0]])
                mm([lambda e, k=k, m=m: e.matmul(PS[1][:, :], lhsT=wg1[:, k, 128 * m:128 * m + 128], rhs=yrhs(k), start=(k == 0), stop=(k == 3)) for k in range(4)],
                   reads=[wg1_b, ygl_b], writes=[PSb[1]])
                op("act", lambda e: e.activation(out=tmpA, in_=PS[1][:, :], func=AF.Sigmoid), reads=[PSb[1]], writes=[tmpA_b])
                op("dve", lambda e, m=m: e.tensor_tensor(out=ys5[:, m, :], in0=PS[0][:, :], in1=tmpA, op=ALU.mult), reads=[PSb[0], tmpA_b], writes=[ys5_b],
                   partial=(m > 0))
            fm_tiles(Wl["w_in"], KT, O_GS5, 1024, xn_rhs,
                     lambda m, bank: op("act", lambda e: e.activation(out=sg5[:, m, :], in_=PS[bank][:, :], func=AF.Sigmoid),
                                        reads=[PSb[bank]], writes=[sg5_b] + ([PB] if m == 0 else []), partial=(m > 0)), rbufs=xn_b)
            fm_tiles(Wl["w_in"], KT, O_GG, 1024, xn_rhs,
                     lambda m, bank: op("act", lambda e: e.activation(out=sgg[:, m, :], in_=PS[bank][:, :], func=AF.Sigmoid),
                                        reads=[PSb[bank]], writes=[sgg_b], partial=(m > 0)), rbufs=xn_b)
            for c2 in range(2):
                ws, ws_b = wload(Wl["w_branch_s5"][:, 512 * c2:512 * c2 + 512], 4, 512)
                wgl, wgl_b = wload(Wl["w_branch_gla"][:, 512 * c2:512 * c2 + 512], 8, 512)
                for t in range(4):
                    m = 4 * c2 + t
                    mm([lambda e, k=k, t=t: e.matmul(PS[0][:, :], lhsT=ws[:, k, 128 * t:128 * t + 128], rhs=ys5[:, k, :], start=(k == 0), stop=(k == 3)) for k in range(4)],
                       reads=[ws_b, ys5_b], writes=[PSb[0]])
                    mm([lambda e, k=k, t=t: e.matmul(PS[1][:, :], lhsT=wgl[:, k, 128 * t:128 * t + 128], rhs=ygla[:, k, :], start=(k == 0), stop=(k == 7)) for k in range(8)],
                       reads=[wgl_b, ygla_b], writes=[PSb[1]])
                    op("dve", lambda e, m=m: e.tensor_tensor(out=tmpA, in0=PS[0][:, :], in1=sg5[:, m, :], op=ALU.mult), reads=[PSb[0], sg5_b], writes=[tmpA_b])
                    op("dve", lambda e, m=m: e.tensor_tensor(out=tmpB, in0=PS[1][:, :], in1=sgg[:, m, :], op=ALU.mult), reads=[PSb[1], sgg_b], writes=[tmpB_b])
                    op("dve", lambda e, m=m: e.tensor_tensor(out=mixed[:, m, :], in0=tmpA, in1=tmpB, op=ALU.add), reads=[tmpA_b, tmpB_b], writes=[mixed_b],
                       partial=(m > 0))
            cast_some(n_per)
            cast_some(n_per)
            hk_b = [Buf("hk%d" % m) for m in range(KT)]
            def out_consume(m, bank):
                op("dve", lambda e: e.tensor_tensor(out=h32[:, m, :], in0=PS[bank][:, :], in1=h32[:, m, :], op=ALU.add),
                   reads=[PSb[bank], h32_b], writes=[h32_b, hk_b[m]], partial=True)
                op("act", lambda e: e.activation(out=sqt[:, m, :], in_=h32[:, m, :], func=AF.Square), reads=[hk_b[m]], writes=[sq_b], partial=(m > 0))
                if m >= 1:
                    ssq_mm(m - 1)

            def ssq_mm(m):
                mm([lambda e: e.matmul(PS[6][:, :], lhsT=ones_b[:, :], rhs=sqt[:, m, :], start=(m == 0), stop=(m == KT - 1))],
                   reads=[sq_b, small_b], writes=[PSb[6]], partial=(m > 0))
            fm_tiles(Wl["w_out"], KT, 0, 1024, lambda k: mixed[:, k, :], out_consume, rbufs=[mixed_b])
            ssq_mm(KT - 1)
            if "hmix" in dump_aps:
                dma("pool", dump_aps["hmix"][:, tsl].rearrange("(k p) t -> p k t", p=128), h32[:, :, :], reads=[h32_b])
            op("act", lambda e: e.activation(out=tmpA, in_=PS[6][:, :], func=AF.Sqrt, scale=1.0 / D, bias=EPS), reads=[PSb[6]], writes=[tmpA_b])
            op("dve", lambda e: e.reciprocal(out=rs2, in_=tmpA), reads=[tmpA_b], writes=[rs2_b])
            make_xn(h32, h32_b, g2col[:, l, :], rs2, rs2_b, xn, xn_b)
            for c0 in range(0, DFF, 256):
                if c0 == 1280:
                    cast_some(n_per)
                wga, wga_b = wload(Wl["w_ffn_gate"][:, c0:c0 + 256], KT, 256)
                wua, wua_b = wload(Wl["w_ffn_up"][:, c0:c0 + 256], KT, 256)
                for t in range(2):
                    j = c0 // 128 + t
                    if j == 0:
                        for k in range(KT):
                            mm([lambda e, k=k, t=t: e.matmul(PS[0][:, :], lhsT=wga[:, k, 0:128], rhs=xn[:, k, :], start=(k == 0), stop=(k == KT - 1))],
                               reads=[wga_b, xn_b[k]], writes=[PSb[0]], partial=(k > 0))
                    else:
                        mm([lambda e, k=k, t=t: e.matmul(PS[0 + 2 * t][:, :], lhsT=wga[:, k, 128 * t:128 * t + 128], rhs=xn[:, k, :], start=(k == 0), stop=(k == KT - 1))
                            for k in range(KT)], reads=[wga_b] + xn_b, writes=[PSb[0 + 2 * t]])
                    mm([lambda e, k=k, t=t: e.matmul(PS[1 + 2 * t][:, :], lhsT=wua[:, k, 128 * t:128 * t + 128], rhs=xn[:, k, :], start=(k == 0), stop=(k == KT - 1))
                        for k in range(KT)], reads=[wua_b] + xn_b, writes=[PSb[1 + 2 * t]])
                    tb, tbb = tbf[t], tbf_b[t]
                    op("act", lambda e, t=t, tb=tb: e.activation(out=tb, in_=PS[0 + 2 * t][:, :], func=AF.Silu), reads=[PSb[0 + 2 * t]], writes=[tbb])
                    op("dve", lambda e, t=t, tb=tb, j=j: e.tensor_tensor(out=hid[:, j, :], in0=PS[1 + 2 * t][:, :], in1=tb, op=ALU.mult),
                       reads=[PSb[1 + 2 * t], tbb], writes=[hid_b], partial=(j > 0))
            cast_some(n_per)
            if b + 1 < NBLK:
                rms_block(h32s[(b + 1) % 2], h32s_b[(b + 1) % 2], g1col[:, l, :], sqt, sq_b, 6, rsN, rsN_b,
                          xns[(b + 1) % 2], xns_b[(b + 1) % 2], tmpN, tmpN_b)
            for m2 in range(4):
                wd0, wd0_b = wload(Wl["w_ffn_down"][0:1408, 256 * m2:256 * m2 + 256], 11, 256)
                wd1, wd1_b = wload(Wl["w_ffn_down"][1408:2816, 256 * m2:256 * m2 + 256], 11, 256)
                for t in range(2):
                    m = 2 * m2 + t
                    bank = 4 + t
                    mm([lambda e, k=k, t=t, bank=bank: e.matmul(PS[bank][:, :], lhsT=(wd0 if k < 11 else wd1)[:, k % 11, 128 * t:128 * t + 128], rhs=hid[:, k, :],
                                                              start=(k == 0), stop=(k == FT - 1)) for k in range(FT)],
                       reads=[wd0_b, wd1_b, hid_b], writes=[PSb[bank]])
                    op("dve", lambda e, m=m, bank=bank: e.tensor_tensor(out=h32[:, m, :], in0=PS[bank][:, :], in1=h32[:, m, :], op=ALU.add),
                       reads=[PSb[bank], h32_b], writes=[h32_b])
            if last and final_norm:
                rms_block(h32, h32_b, gfcol, sqt, sq_b, 6, rs2, rs2_b, None, None, tmpA, tmpA_b)
                for k in range(KT):
                    op("dve", lambda e, k=k: e.scalar_tensor_tensor(out=h32[:, k, :], in0=h32[:, k, :], scalar=gfcol[:, k:k + 1], in1=rs2,
                                                                  op0=ALU.mult, op1=ALU.mult), reads=[h32_b, rs2_b, small_b], writes=[h32_b])
            dst_d = outT if last else hT
            dma("pool", dst_d[:, tsl].rearrange("(k p) t -> p k t", p=128), h32[:, :, :], reads=[h32_b], writes=[hd_b[b]])
            if b == NBLK - 1:
                cast_some(len(cast_q))

    except _Stop:
        pass
    kb.barrier(full=True)
    es.close()
    return nc, kb


_CACHE = {}


def kernel(**inputs):
    x = np.asarray(inputs["x"], dtype=np.float32)
    B = x.shape[0]
    if "nc" not in _CACHE:
        _CACHE["nc"] = build_program()[0]
    nc = _CACHE["nc"]
    shared = {k: np.ascontiguousarray(np.asarray(v, dtype=np.float32)) for k, v in inputs.items() if k != "x"}
    shared["consts"] = CONSTS_NP
    in_maps = []
    for b in range(B):
        m = dict(shared)
        m["xT"] = np.ascontiguousarray(x[b].T)
        in_maps.append(m)
    res = run_bass_kernel_spmd(nc, in_maps, core_ids=list(range(B)))
    out = np.stack([np.ascontiguousarray(res.results[b]["outT"].T) for b in range(B)], axis=0)
    return out.astype(np.float32)
```
